# Optimizing a Trainium2 kernel written in Bass

```python
import jax, jax.numpy as jnp
from jax import lax
import numpy as np

D_MODEL = 1024
BATCH = 8
SEQ = 2048
DEPTH = 2
DEC_BATCH = 128
DEC_SEQ = 1
PAST_LEN = 16384
PAGE_SIZE = 128

N_MIXERS = 2
N_LAYERS_A = (DEPTH + 1) // 2
N_LAYERS_B = DEPTH // 2
D_FF = 2816
CONV_A_WIDTH = 31
CONV_B_WIDTH = 3
RMS_EPS = 1e-6
LN_EPS = 1e-5
FFN_RES_WEIGHT = 0.5

kernel_name = "hybrid_conformer_shortconv_decoder_step"


def rms_norm(x, g):
    xf = x.astype(jnp.float32)
    y = xf * lax.rsqrt(jnp.mean(xf * xf, axis=-1, keepdims=True) + RMS_EPS)
    return (y * g.astype(jnp.float32)).astype(x.dtype)


def layer_norm(x, g, b):
    xf = x.astype(jnp.float32)
    mu = jnp.mean(xf, axis=-1, keepdims=True)
    var = jnp.mean(jnp.square(xf - mu), axis=-1, keepdims=True)
    y = (xf - mu) * lax.rsqrt(var + LN_EPS)
    return (y * g.astype(jnp.float32) + b.astype(jnp.float32)).astype(x.dtype)


def swiglu(x, w_gate, w_up, w_down):
    return (jax.nn.silu(x @ w_gate) * (x @ w_up)) @ w_down


def causal_dwconv(u, buf, w):
    width, ch = w.shape
    full = jnp.concatenate([buf.astype(u.dtype), u], axis=1)
    out = lax.conv_general_dilated(
        full, w.astype(u.dtype)[:, None, :], window_strides=(1,), padding="VALID",
        dimension_numbers=("NWC", "WIO", "NWC"), feature_group_count=ch)
    return out, full[:, full.shape[1] - (width - 1):]


def conformer_conv_mixer(x, buf, w_pw1, b_pw1, w_dw, b_dw, ln_g, ln_b, w_pw2, b_pw2):
    h = x @ w_pw1 + b_pw1
    a, g = jnp.split(h, 2, axis=-1)
    v = a * jax.nn.sigmoid(g)
    c, new_buf = causal_dwconv(v, buf, w_dw)
    c = jax.nn.silu(layer_norm(c + b_dw, ln_g, ln_b))
    return c @ w_pw2 + b_pw2, new_buf


def short_gated_conv_mixer(x, buf, w_in, w_conv, w_out):
    bch = x @ w_in
    b_gate, c_gate, h = jnp.split(bch, 3, axis=-1)
    c, new_buf = causal_dwconv(c_gate * h, buf, w_conv)
    return (b_gate * c) @ w_out, new_buf


def run_trunk(x, bufs_a, bufs_b, ffn1_norm, ffn1_w_gate, ffn1_w_up, ffn1_w_down,
              mix_norm, ffn2_norm, ffn2_w_gate, ffn2_w_up, ffn2_w_down, final_norm,
              a_w_pw1, a_b_pw1, a_w_dw, a_b_dw, a_ln_g, a_ln_b, a_w_pw2, a_b_pw2,
              b_w_in, b_w_conv, b_w_out):
    new_a, new_b = [], []
    for i in range(DEPTH):
        x = x + FFN_RES_WEIGHT * swiglu(rms_norm(x, ffn1_norm[i]), ffn1_w_gate[i], ffn1_w_up[i], ffn1_w_down[i])
        hn = rms_norm(x, mix_norm[i])
        j = i // N_MIXERS
        if i % N_MIXERS == 0:
            m, nb = conformer_conv_mixer(hn, bufs_a[j], a_w_pw1[j], a_b_pw1[j], a_w_dw[j], a_b_dw[j],
                                         a_ln_g[j], a_ln_b[j], a_w_pw2[j], a_b_pw2[j])
            new_a.append(nb)
        else:
            m, nb = short_gated_conv_mixer(hn, bufs_b[j], b_w_in[j], b_w_conv[j], b_w_out[j])
            new_b.append(nb)
        x = x + m
        x = x + FFN_RES_WEIGHT * swiglu(rms_norm(x, ffn2_norm[i]), ffn2_w_gate[i], ffn2_w_up[i], ffn2_w_down[i])
    return rms_norm(x, final_norm), jnp.stack(new_a), jnp.stack(new_b)


def setup_inputs(seed: int = 0) -> dict:
    key = jax.random.key(seed)
    ks = iter(jax.random.split(key, 32))
    f32 = jnp.float32
    D = D_MODEL

    def nrm(shape, scale):
        return jax.random.normal(next(ks), shape, f32) * scale

    def gain(shape):
        return 1.0 + 0.02 * jax.random.normal(next(ks), shape, f32)

    return {
        "x_prompt": nrm((BATCH, SEQ, D), 1.0),
        "x_sample": nrm((DEC_BATCH, DEC_SEQ, D), 1.0),
        "state_conv_a": nrm((N_LAYERS_A, DEC_BATCH, CONV_A_WIDTH - 1, D), 1.0),
        "state_conv_b": nrm((N_LAYERS_B, DEC_BATCH, CONV_B_WIDTH - 1, D), 1.0),
        "ffn1_norm": gain((DEPTH, D)),
        "ffn1_w_gate": nrm((DEPTH, D, D_FF), D ** -0.5),
        "ffn1_w_up": nrm((DEPTH, D, D_FF), D ** -0.5),
        "ffn1_w_down": nrm((DEPTH, D_FF, D), D_FF ** -0.5),
        "mix_norm": gain((DEPTH, D)),
        "ffn2_norm": gain((DEPTH, D)),
        "ffn2_w_gate": nrm((DEPTH, D, D_FF), D ** -0.5),
        "ffn2_w_up": nrm((DEPTH, D, D_FF), D ** -0.5),
        "ffn2_w_down": nrm((DEPTH, D_FF, D), D_FF ** -0.5),
        "final_norm": gain((D,)),
        "a_w_pw1": nrm((N_LAYERS_A, D, 2 * D), D ** -0.5),
        "a_b_pw1": nrm((N_LAYERS_A, 2 * D), 0.02),
        "a_w_dw": nrm((N_LAYERS_A, CONV_A_WIDTH, D), CONV_A_WIDTH ** -0.5),
        "a_b_dw": nrm((N_LAYERS_A, D), 0.02),
        "a_ln_g": gain((N_LAYERS_A, D)),
        "a_ln_b": nrm((N_LAYERS_A, D), 0.02),
        "a_w_pw2": nrm((N_LAYERS_A, D, D), D ** -0.5),
        "a_b_pw2": nrm((N_LAYERS_A, D), 0.02),
        "b_w_in": nrm((N_LAYERS_B, D, 3 * D), D ** -0.5),
        "b_w_conv": nrm((N_LAYERS_B, CONV_B_WIDTH, D), CONV_B_WIDTH ** -0.5),
        "b_w_out": nrm((N_LAYERS_B, D, D), D ** -0.5),
    }


def reference(x_prompt, x_sample, state_conv_a, state_conv_b,
              ffn1_norm, ffn1_w_gate, ffn1_w_up, ffn1_w_down,
              mix_norm, ffn2_norm, ffn2_w_gate, ffn2_w_up, ffn2_w_down, final_norm,
              a_w_pw1, a_b_pw1, a_w_dw, a_b_dw, a_ln_g, a_ln_b, a_w_pw2, a_b_pw2,
              b_w_in, b_w_conv, b_w_out):
    weights = (ffn1_norm, ffn1_w_gate, ffn1_w_up, ffn1_w_down,
               mix_norm, ffn2_norm, ffn2_w_gate, ffn2_w_up, ffn2_w_down, final_norm,
               a_w_pw1, a_b_pw1, a_w_dw, a_b_dw, a_ln_g, a_ln_b, a_w_pw2, a_b_pw2,
               b_w_in, b_w_conv, b_w_out)
    zeros_a = jnp.zeros((N_LAYERS_A, x_prompt.shape[0], CONV_A_WIDTH - 1, D_MODEL), x_prompt.dtype)
    zeros_b = jnp.zeros((N_LAYERS_B, x_prompt.shape[0], CONV_B_WIDTH - 1, D_MODEL), x_prompt.dtype)
    y_prompt, new_conv_a_prompt, new_conv_b_prompt = run_trunk(x_prompt, zeros_a, zeros_b, *weights)
    y_sample, new_conv_a_sample, new_conv_b_sample = run_trunk(x_sample, state_conv_a, state_conv_b, *weights)
    return (y_prompt, y_sample, new_conv_a_prompt, new_conv_a_sample, new_conv_b_prompt, new_conv_b_sample)
```

```python
import contextlib
import numpy as np
import concourse.bass as bass
import concourse.mybir as mybir
from concourse.bass_utils import run_bass_kernel_spmd

F32 = mybir.dt.float32
BF16 = mybir.dt.bfloat16
U8 = mybir.dt.uint8
AF = mybir.ActivationFunctionType
ALU = mybir.AluOpType
AX = mybir.AxisListType

NCORES = 8
D = 1024
DFF = 2816
SEQ = 2048
NSAMP = 16
TT = SEQ + NSAMP
KC = 8
CA_W = 31
CB_W = 3
TILES = [(0, 512), (512, 512), (1024, 512), (1536, 512), (2048, 16)]
FFN_GROUPS = [(0, 6), (6, 8), (14, 8)]
RMS_EPS = 1e-6
LN_EPS = 1e-5

C_FFN1N = (0, 8)
C_MIXN = (16, 24)
C_FFN2N = (32, 40)
C_FINN = 48
C_BPW1 = 56
C_BDW = 72
C_LNG = 80
C_LNB = 88
C_BPW2 = 96
C_WDW = 104
C_WCB = 352
C_ID32 = 376
C_EPS_RMS = 408
C_EPS_LN = 409
NCONST = 416

NSLOT = 5
SLOT_ELEMS = 2048
O_X = 0
O_XN = O_X + KC * TT * 4
O_HB = O_XN + KC * TT * 2
HB_W = 2080
O_WR = O_HB + KC * HB_W * 2
O_CONST = O_WR + NSLOT * SLOT_ELEMS * 2
O_ONES = O_CONST + NCONST * 4
O_SCR = O_ONES + 256
O_ST = O_SCR
O_RS = O_ST + 4096
O_SQ = O_RS + 4096
O_CB = O_SQ + 8192
O_DG = O_CB + 16384
O_SM = O_DG + 15872
O_VLAST = O_SM
O_VS = O_VLAST + 960
O_SA = O_VS + 512
O_SO = O_SA + 3840
O_TS = O_SO + 1920
O_CSB = O_TS + 128
O_QS = O_CSB + 512
ARENA = O_QS + 128


class Prog:
    ENGS = ("pe", "act", "dve", "pool", "sp")

    def __init__(self):
        self.ops = {e: [] for e in self.ENGS}
        self.cnt = {}
        self.waited = {e: {} for e in self.ENGS}

    def tick(self, sem, amount):
        self.cnt[sem] = self.cnt.get(sem, 0) + amount
        return (sem, self.cnt[sem])

    def wait(self, eng, tk):
        if tk is None:
            return
        sem, val = tk
        if self.waited[eng].get(sem, 0) >= val:
            return
        self.waited[eng][sem] = val
        self.ops[eng].append(("wait", sem, val))

    def emit(self, eng, fn, waits=(), sem="auto", amount=1):
        for w in waits:
            self.wait(eng, w)
        if sem == "auto":
            sem = "p_" + eng
        tk = None
        if sem is not None:
            tk = self.tick(sem, amount)
        self.ops[eng].append(("op", fn, (sem, amount) if sem is not None else None))
        return tk


def build_program(stop_after=6):
    nc = bass.Bass("TRN2", target_bir_lowering=False)
    P = Prog()

    def din(name, shape):
        return nc.dram_tensor(name, list(shape), F32, kind="ExternalInput").ap()

    def dout(name, shape):
        return nc.dram_tensor(name, list(shape), F32, kind="ExternalOutput").ap()

    xT = din("xT", [D, TT])
    consts = din("consts", [128, NCONST])
    sA = din("sA", [D, NSAMP * 30])
    sB = din("sB", [D, NSAMP * 2])
    w_f1g = din("ffn1_w_gate", [2, D, DFF])
    w_f1u = din("ffn1_w_up", [2, D, DFF])
    w_f1d = din("ffn1_w_down", [2, DFF, D])
    w_f2g = din("ffn2_w_gate", [2, D, DFF])
    w_f2u = din("ffn2_w_up", [2, D, DFF])
    w_f2d = din("ffn2_w_down", [2, DFF, D])
    w_pw1 = din("a_w_pw1", [D, 2 * D])
    w_pw2 = din("a_w_pw2", [D, D])
    w_bin = din("b_w_in", [D, 3 * D])
    w_bout = din("b_w_out", [D, D])
    yT = dout("yT", [D, TT])
    nap = dout("nap", [D, 30])
    nas = dout("nas", [D, NSAMP * 30])
    nbp = dout("nbp", [D, 2])
    nbs = dout("nbs", [D, NSAMP * 2])

    es = contextlib.ExitStack()
    arena = es.enter_context(nc.sbuf_tensor("arena", [128, ARENA], U8))
    psum = es.enter_context(nc.psum_tensor("ps", [128, 8, 512], F32))

    def view(off, shape, dt):
        isz = 4 if dt == F32 else 2
        n = int(np.prod(shape))
        v = arena[:, off:off + n * isz].bitcast(dt)
        if len(shape) == 1:
            return v
        if len(shape) == 2:
            return v.rearrange("p (a b) -> p a b", a=shape[0])
        if len(shape) == 3:
            return v.rearrange("p (a b c) -> p a b c", a=shape[0], b=shape[1])
        raise ValueError

    X = view(O_X, [KC, TT], F32)
    XN = view(O_XN, [KC, TT], BF16)
    HB = view(O_HB, [KC, HB_W], BF16)
    WR = view(O_WR, [NSLOT, SLOT_ELEMS], BF16)
    CONST = view(O_CONST, [NCONST], F32)
    ONES = view(O_ONES, [128], BF16)
    ST = view(O_ST, [2, 512], F32)
    RS = view(O_RS, [2, 512], F32)
    SQ = view(O_SQ, [KC, 512], BF16)
    CB = view(O_CB, [KC, 512], F32)
    DGB = view(O_DG, [KC, CA_W, 32], BF16)
    VLAST = view(O_VLAST, [KC, 30], F32)
    VS = view(O_VS, [KC, NSAMP], F32)
    SA = view(O_SA, [2, NSAMP, 30], F32)
    SO = view(O_SO, [NSAMP, 30], F32)
    CSB = view(O_CSB, [KC, NSAMP], F32)
    SQ_S = view(O_CSB, [KC, NSAMP], BF16)
    CBB_S = view(O_QS, [2, NSAMP], BF16)
    CQ_S = view(O_QS + 64, [2, NSAMP], BF16)
    UB = view(O_CB, [2, 514], F32)
    TA = view(O_CB + 4160, [512], F32)
    TB = view(O_CB + 6208, [512], F32)
    US = view(O_CB + 10304, [KC, NSAMP], F32)
    UBL = view(O_CB + 10816, [KC, 2], F32)
    SBO = view(O_CB + 10880, [KC, NSAMP * 2], F32)
    TD = view(O_CB + 12288, [2, 512], F32)
    SBI = view(O_CB + 8256, [KC, NSAMP * 2], F32)
    CBB = view(O_SQ, [2, 512], BF16)
    CQ = view(O_SQ + 2048, [2, 512], BF16)
    MV = view(O_SQ + 4096, [2, 512], F32)

    def ccol(c0, n=1):
        return CONST[:, c0:c0 + n]

    class Banks:
        def __init__(self):
            self.free = [[] for _ in range(8)]
            self.rr = 0
            self.srr = 0

        def get(self, n):
            out = []
            for _ in range(n):
                out.append(self.rr)
                self.rr = (self.rr + 1) % 6
            return out

        def stat(self):
            b = 6 + self.srr
            self.srr ^= 1
            return b

        def wait_free(self, b):
            for tk in self.free[b]:
                P.wait("pe", tk)
            self.free[b] = []

        def release(self, b, *tks):
            self.free[b] = [t for t in tks if t is not None]

    PS = Banks()

    NEXTRA = 3
    WRX = view(O_DG, [NEXTRA, SLOT_ELEMS], BF16)

    def slot_view(slot, shape):
        n = int(np.prod(shape))
        base = WR[:, slot, 0:n] if slot < NSLOT else WRX[:, slot - NSLOT, 0:n]
        return base.rearrange("p (a b) -> p a b", a=shape[0])

    class WRing:
        def __init__(self):
            self.units = []
            self.next_dma = 0
            self.free_slots = [(i, None) for i in range(NSLOT)]
            self.slot_of = {}
            self.load_tk = {}
            self.cursor = 0
            self.stolen = None

        def plan(self, src, shape, tag):
            self.units.append((src, shape, tag))

        def enable_extra(self, tk):
            for i in range(NEXTRA):
                self.free_slots.append((NSLOT + i, tk))
            self.topup()

        def topup(self):
            while self.next_dma < len(self.units) and self.free_slots:
                v = self.next_dma
                slot, ftk = self.free_slots.pop(0)
                src, shape, _ = self.units[v]
                dst = slot_view(slot, shape)
                prio = ld_x[0] if v == 0 else (ld_x[4] if v == 2 else None)
                tk = P.emit("pool", (lambda e, dst=dst, src=src: e.dma_start(out=dst, in_=src)),
                            waits=[ftk, prio], sem="w%d" % slot, amount=16)
                self.load_tk[v] = tk
                self.slot_of[v] = slot
                self.next_dma += 1

        def acquire(self, tag):
            u = self.cursor
            assert self.units[u][2] == tag, (self.units[u][2], tag)
            self.cursor += 1
            self.topup()
            assert u in self.load_tk, "weight ring exhausted (too many units held)"
            return u, slot_view(self.slot_of[u], self.units[u][1]), self.load_tk[u]

        def release(self, u, tk, steal=False):
            if steal:
                self.stolen = (self.slot_of[u], tk)
                return
            self.free_slots.append((self.slot_of[u], tk))
            self.topup()

        def give_back(self, tk):
            self.free_slots.append((self.stolen[0], tk))
            self.stolen = None
            self.topup()

    W = WRing()
    ld_x = []
    x_ld = []

    def colblock(wmat, c0, width=256):
        return wmat[:, c0:c0 + width].rearrange("(k p) f -> p k f", p=128)

    def rowblock(wmat, r0, nrows_chunks, c0, width=256):
        return wmat[r0 * 128:(r0 + nrows_chunks) * 128, c0:c0 + width].rearrange("(r p) f -> p r f", p=128)

    def plan_ffn(name, wg, wu, wd):
        for (j0, n) in FFN_GROUPS:
            for jp in range(0, n, 2):
                W.plan(colblock(wg, (j0 + jp) * 128), [KC, 256], name + "g%d" % (j0 + jp))
                W.plan(colblock(wu, (j0 + jp) * 128), [KC, 256], name + "u%d" % (j0 + jp))
            for mp in range(4):
                W.plan(rowblock(wd, j0, n, mp * 256), [n, 256], name + "d%d_%d" % (j0, mp))

    phases = ["ffn1_0", "mixA", "ffn2_0", "ffn1_1", "mixB", "ffn2_1", "final"]
    if stop_after >= 0:
        plan_ffn("f10", w_f1g[0], w_f1u[0], w_f1d[0])
    if stop_after >= 1:
        for cp in range(4):
            W.plan(colblock(w_pw1, cp * 256), [KC, 256], "pw1a%d" % cp)
            W.plan(colblock(w_pw1, 1024 + cp * 256), [KC, 256], "pw1g%d" % cp)
        for mp in range(4):
            W.plan(colblock(w_pw2, mp * 256), [KC, 256], "pw2_%d" % mp)
    if stop_after >= 2:
        plan_ffn("f20", w_f2g[0], w_f2u[0], w_f2d[0])
    if stop_after >= 3:
        plan_ffn("f11", w_f1g[1], w_f1u[1], w_f1d[1])
    if stop_after >= 4:
        for cp in range(4):
            W.plan(colblock(w_bin, 1024 + cp * 256), [KC, 256], "binc%d" % cp)
            W.plan(colblock(w_bin, 2048 + cp * 256), [KC, 256], "binh%d" % cp)
            W.plan(colblock(w_bin, cp * 256), [KC, 256], "binb%d" % cp)
        for mp in range(4):
            W.plan(colblock(w_bout, mp * 256), [KC, 256], "bout%d" % mp)
    if stop_after >= 5:
        plan_ffn("f21", w_f2g[1], w_f2u[1], w_f2d[1])

    x_tk = [None] * 5
    x_w = {}
    xn_tk = [None] * 5
    st_free = [None, None]
    st_rr = [0]
    rs_free = [None]
    sq_free = [None]
    sqs_free = [None]
    out_tks = []

    def st_next():
        s = st_rr[0]
        st_rr[0] ^= 1
        return s

    c_tk = P.emit("sp", lambda e: e.dma_start(out=CONST, in_=consts), sem="ld_c", amount=16)
    xsrc = xT.rearrange("(c p) t -> p c t", p=128)
    for ti, (t0, w) in enumerate(TILES):
        x_tk[ti] = P.emit("sp", (lambda e, t0=t0, w=w: e.dma_start(out=X[:, :, t0:t0 + w], in_=xsrc[:, :, t0:t0 + w])),
                          sem="ld_x%d" % ti, amount=16)
        ld_x.append(x_tk[ti])
        x_ld.append(x_tk[ti])
    ones_tk = P.emit("dve", lambda e: e.memset(ONES, 1.0))
    P.wait("act", c_tk)
    P.wait("dve", c_tk)
    P.wait("pool", c_tk)
    P.wait("pe", ones_tk)

    class Norm:
        def __init__(self, gcol, final=False):
            self.gcol = gcol
            self.final = final
            self.stat = {}
            self.sq = {}
            self.sqbuf = {}
            self.alt_sq = False
            self.override = {}

        def part1a(self, ti):
            t0, w = TILES[ti]
            if ti in self.override:
                ob = self.override[ti]
                a1 = P.emit("act", (lambda e, t0=t0, w=w, ob=ob: e.activation(out=ob[:, :, 0:w], in_=X[:, :, t0:t0 + w], func=AF.Square)),
                            waits=[x_tk[ti]])
                self.sqbuf[ti] = ob
                self.sq[ti] = a1
                return
            if ti == 4:
                a1 = P.emit("act", (lambda e, t0=t0, w=w: e.activation(out=SQ_S, in_=X[:, :, t0:t0 + w], func=AF.Square)),
                            waits=[x_tk[ti], sqs_free[0]])
            elif self.alt_sq:
                a1 = P.emit("act", (lambda e, t0=t0, w=w: e.activation(out=HB[:, :, 1536:1536 + w], in_=X[:, :, t0:t0 + w], func=AF.Square)),
                            waits=[x_tk[ti], sq_free[0]])
            else:
                a1 = P.emit("act", (lambda e, t0=t0, w=w: e.activation(out=SQ[:, :, 0:w], in_=X[:, :, t0:t0 + w], func=AF.Square)),
                            waits=[x_tk[ti], sq_free[0], sq_guard[0]])
            self.sqbuf[ti] = SQ_S if ti == 4 else (HB[:, :, 1536:2048] if self.alt_sq else SQ)
            self.sq[ti] = a1

        def part1b(self, ti):
            t0, w = TILES[ti]
            a1 = self.sq.pop(ti)
            bank = PS.stat()
            PS.wait_free(bank)
            P.wait("pe", a1)
            p1 = None
            sqb = self.sqbuf.pop(ti)
            for c in range(KC):
                p1 = P.emit("pe", (lambda e, c=c, w=w, bank=bank, sqb=sqb: e.matmul(psum[:, bank, 0:w], ONES, sqb[:, c, 0:w],
                                                                                     start=(c == 0), stop=(c == KC - 1))),
                            sem="auto" if c == KC - 1 else None)
            if ti == 4:
                sqs_free[0] = p1
            else:
                sq_free[0] = p1
            self.stat[ti] = (bank, p1)

        def part1(self, ti):
            self.part1a(ti)
            self.part1b(ti)

        def part2(self, ti):
            t0, w = TILES[ti]
            bank, p1 = self.stat.pop(ti)
            gcol = self.gcol
            a2 = P.emit("act", (lambda e, w=w, bank=bank: e.activation(out=RS[:, 0, 0:w], in_=psum[:, bank, 0:w], func=AF.Ln,
                                                                         scale=1.0 / D, bias=ccol(C_EPS_RMS))),
                        waits=[p1, rs_free[0]])
            PS.release(bank, a2)
            P.wait("act", a2)
            d1 = P.emit("act", (lambda e, w=w: e.activation(out=RS[:, 1, 0:w], in_=RS[:, 0, 0:w], func=AF.Exp, scale=-0.5)),
                        waits=[x_tk[ti]])
            P.wait("dve", d1)
            d = None
            dst = X if self.final else XN
            for c in range(KC):
                d = P.emit("dve", (lambda e, c=c, t0=t0, w=w, dst=dst: e.scalar_tensor_tensor(
                    out=dst[:, c, t0:t0 + w], in0=X[:, c, t0:t0 + w], scalar=ccol(gcol + c), in1=RS[:, 1, 0:w],
                    op0=ALU.mult, op1=ALU.mult)))
                if self.final and c % 2 == 1:
                    out_tks.append(P.emit("sp", (lambda e, c=c, t0=t0, w=w: e.dma_start(
                        out=ydst[:, c - 1:c + 1, t0:t0 + w], in_=X[:, c - 1:c + 1, t0:t0 + w])),
                        waits=[d], sem="out", amount=16))
            rs_free[0] = d
            if self.final:
                x_tk[ti] = d
            else:
                xn_tk[ti] = d

        def all(self):
            for ti in range(5):
                self.part1(ti)
                self.part2(ti)

    deferred = []
    sq_guard = [None]

    def flush_deferred():
        if deferred:
            nrm, ti, full = deferred.pop(0)
            if full:
                nrm.part1a(ti)
            nrm.part1b(ti)
            nrm.part2(ti)

    FINAL_ORDER = [4, 0, 1, 2, 3]
    NORMAL_ORDER = [(jj, ti) for jj in range(2) for ti in range(5)]
    FIRST_ORDER = [(0, 0), (0, 1), (1, 0), (1, 1), (0, 2), (0, 3), (0, 4), (1, 2), (1, 3), (1, 4)]

    def tiled_final_stage(emit_tile, nxt, hooks=None, order=None):
        prev = None
        order = order or FINAL_ORDER
        early = set()
        for idx, ti in enumerate(order):
            emit_tile(ti)
            if nxt is not None and prev is not None and prev not in early:
                nxt.part1a(prev)
            if nxt is not None and ti == 4:
                nxt.part1a(ti)
                early.add(ti)
            if hooks is not None and idx in hooks:
                hooks[idx]()
            if nxt is not None and prev is not None:
                nxt.part1b(prev)
                nxt.part2(prev)
            prev = ti
        if nxt is not None:
            if prev not in early:
                nxt.part1a(prev)
            deferred.append((nxt, prev, False))

    def ffn(name, nxt):
        last_pe = None
        for gi, (j0, n) in enumerate(FFN_GROUPS):
            h_tk = [None] * 5
            for jp in range(0, n, 2):
                ug, wgv, ltg = W.acquire(name + "g%d" % (j0 + jp))
                uu, wuv, ltu = W.acquire(name + "u%d" % (j0 + jp))
                first = (gi == 0 and jp == 0)
                for it_i, (jj, ti) in enumerate(FIRST_ORDER if first else NORMAL_ORDER):
                    if True:
                        hj = jp + jj
                        t0, w = TILES[ti]
                        bg, bu = PS.get(2)
                        P.wait("pe", xn_tk[ti])
                        P.wait("pe", ltg)
                        PS.wait_free(bg)
                        for k in range(KC):
                            P.emit("pe", (lambda e, k=k, jj=jj, t0=t0, w=w, bg=bg, wgv=wgv: e.matmul(
                                psum[:, bg, 0:w], wgv[:, k, jj * 128:(jj + 1) * 128], XN[:, k, t0:t0 + w],
                                start=(k == 0), stop=(k == KC - 1))), sem=None)
                        P.wait("pe", ltu)
                        PS.wait_free(bu)
                        for k in range(KC):
                            pk = P.emit("pe", (lambda e, k=k, jj=jj, t0=t0, w=w, bu=bu, wuv=wuv: e.matmul(
                                psum[:, bu, 0:w], wuv[:, k, jj * 128:(jj + 1) * 128], XN[:, k, t0:t0 + w],
                                start=(k == 0), stop=(k == KC - 1))), sem="auto" if k == KC - 1 else None)
                        s = st_next()
                        a = P.emit("act", (lambda e, s=s, w=w, bg=bg: e.activation(out=ST[:, s, 0:w], in_=psum[:, bg, 0:w], func=AF.Silu)),
                                   waits=[pk, st_free[s]])
                        dd = P.emit("dve", (lambda e, s=s, w=w, bu=bu, hj=hj, t0=t0: e.tensor_tensor(
                            out=HB[:, hj, t0:t0 + w], in0=ST[:, s, 0:w], in1=psum[:, bu, 0:w], op=ALU.mult)), waits=[a])
                        PS.release(bg, a)
                        PS.release(bu, dd)
                        st_free[s] = dd
                        h_tk[ti] = dd if h_tk[ti] is None or dd[1] > h_tk[ti][1] else h_tk[ti]
                        last_pe = pk
                        if it_i in (1, 2) or (it_i == 0 and len(deferred) > 1):
                            flush_deferred()
                W.release(ug, last_pe)
                W.release(uu, last_pe)

            def down_item(m, ti, wdv, lt):
                nonlocal last_pe
                t0, w = TILES[ti]
                mm = m % 2
                (b,) = PS.get(1)
                PS.wait_free(b)
                P.wait("pe", h_tk[ti])
                P.wait("pe", lt)
                pk = None
                for jj in range(n):
                    pk = P.emit("pe", (lambda e, jj=jj, mm=mm, t0=t0, w=w, b=b, wdv=wdv, n=n: e.matmul(
                        psum[:, b, 0:w], wdv[:, jj, mm * 128:(mm + 1) * 128], HB[:, jj, t0:t0 + w],
                        start=(jj == 0), stop=(jj == n - 1))), sem="auto" if jj == n - 1 else None)
                dd = P.emit("dve", (lambda e, m=m, t0=t0, w=w, b=b: e.scalar_tensor_tensor(
                    out=X[:, m, t0:t0 + w], in0=psum[:, b, 0:w], scalar=0.5, in1=X[:, m, t0:t0 + w],
                    op0=ALU.mult, op1=ALU.add)), waits=[x_w.get((m, ti), x_ld[ti]), pk])
                PS.release(b, dd)
                x_w[(m, ti)] = dd
                x_tk[ti] = dd
                last_pe = pk

            if gi < len(FFN_GROUPS) - 1:
                for mp in range(4):
                    ud, wdv, ltd = W.acquire(name + "d%d_%d" % (j0, mp))
                    for mm in range(2):
                        for ti in range(5):
                            down_item(mp * 2 + mm, ti, wdv, ltd)
                    W.release(ud, last_pe)
            else:
                us = [W.acquire(name + "d%d_%d" % (j0, mp)) for mp in range(4)]

                def tile_fn(ti):
                    for m in range(KC):
                        _, wdv, ltd = us[m // 2]
                        down_item(m, ti, wdv, ltd)
                tiled_final_stage(tile_fn, nxt)
                for (ud, _, _) in us:
                    W.release(ud, last_pe)
        return last_pe

    def mixer_a(prev_pe, nxt):
        ID32 = CONST[:, C_ID32:C_ID32 + 32]
        WDW = CONST[:, C_WDW:C_WDW + KC * CA_W].rearrange("p (c k) -> p c k", c=KC)
        zp = P.emit("dve", lambda e: e.memset(HB[:, :, 0:30], 0.0), waits=[prev_pe])
        v_tk = zp
        last_pe = None
        sa_ld = [None, None]
        sa_free = [None, None]
        so_free = [None]
        cs_tk = [None]

        def sa_load(c):
            sl = c % 2
            sa_ld[sl] = P.emit("sp", (lambda e, sl=sl, c=c: e.dma_start(
                out=SA[:, sl].rearrange("p s k -> p (s k)"), in_=sA[c * 128:(c + 1) * 128, :])),
                waits=[sa_free[sl]], sem="ld_sa%d" % sl, amount=16)

        def sample_conv(c, vs_tk):
            sl = c % 2
            d0 = P.emit("dve", (lambda e, sl=sl, c=c: e.tensor_tensor(
                out=SO[:, :, :], in0=SA[:, sl], in1=WDW[:, c, 0:30].unsqueeze(1).broadcast_to([128, NSAMP, 30]),
                op=ALU.mult)), waits=[sa_ld[sl], so_free[0]])
            P.wait("dve", d0)
            d1 = P.emit("dve", (lambda e: e.tensor_reduce(out=TA_s, in_=SO[:, :, :], axis=AX.X, op=ALU.add)))
            P.wait("dve", d1)
            d2 = P.emit("dve", (lambda e, c=c: e.scalar_tensor_tensor(
                out=TB_s, in0=VS[:, c, :], scalar=WDW[:, c, 30:31], in1=TA_s, op0=ALU.mult, op1=ALU.add)), waits=[vs_tk])
            P.wait("dve", d2)
            cs_tk[0] = P.emit("dve", (lambda e, c=c: e.tensor_scalar(
                out=CSB[:, c, :], in0=TB_s, scalar1=ccol(C_BDW + c), scalar2=None, op0=ALU.add)), waits=[sqs_free[0]])
            a1 = P.emit("act", (lambda e, sl=sl: e.activation(out=SO[:, :, 0:29], in_=SA[:, sl, :, 1:30], func=AF.Copy)),
                        waits=[d1])
            a2 = P.emit("act", (lambda e, c=c: e.activation(out=SO[:, :, 29], in_=VS[:, c, :], func=AF.Copy)), waits=[vs_tk])
            sa_free[sl] = a1
            o = P.emit("sp", (lambda e, c=c: e.dma_start(out=nas[c * 128:(c + 1) * 128, :],
                                                         in_=SO.rearrange("p s k -> p (s k)"))),
                       waits=[a2], sem="so", amount=16)
            so_free[0] = o
            out_tks.append(o)
            if c + 2 < KC:
                sa_load(c + 2)

        sa_load(0)
        sa_load(1)
        dgb_tk = None
        for c_ in range(KC):
            dgb_tk = P.emit("dve", (lambda e, c_=c_: e.tensor_tensor(
                out=DGB[:, c_], in0=ID32.unsqueeze(1).broadcast_to([128, CA_W, 32]),
                in1=WDW[:, c_, :].unsqueeze(2).broadcast_to([128, CA_W, 32]), op=ALU.mult)), waits=[prev_pe])
        vp_tk = [None] * KC
        for cp in range(4):
            ua, wav, lta = W.acquire("pw1a%d" % cp)
            ug, wgv, ltg = W.acquire("pw1g%d" % cp)
            for it_i, (cc, ti) in enumerate(FIRST_ORDER if cp == 0 else NORMAL_ORDER):
                if True:
                    c = cp * 2 + cc
                    t0, w = TILES[ti]
                    ba, bg = PS.get(2)
                    P.wait("pe", xn_tk[ti])
                    P.wait("pe", lta)
                    PS.wait_free(ba)
                    for k in range(KC):
                        P.emit("pe", (lambda e, k=k, cc=cc, t0=t0, w=w, ba=ba, wav=wav: e.matmul(
                            psum[:, ba, 0:w], wav[:, k, cc * 128:(cc + 1) * 128], XN[:, k, t0:t0 + w],
                            start=(k == 0), stop=(k == KC - 1))), sem=None)
                    P.wait("pe", ltg)
                    PS.wait_free(bg)
                    for k in range(KC):
                        pk = P.emit("pe", (lambda e, k=k, cc=cc, t0=t0, w=w, bg=bg, wgv=wgv: e.matmul(
                            psum[:, bg, 0:w], wgv[:, k, cc * 128:(cc + 1) * 128], XN[:, k, t0:t0 + w],
                            start=(k == 0), stop=(k == KC - 1))), sem="auto" if k == KC - 1 else None)
                    s = st_next()
                    a = P.emit("act", (lambda e, s=s, w=w, bg=bg, c=c: e.activation(
                        out=ST[:, s, 0:w], in_=psum[:, bg, 0:w], func=AF.Sigmoid, bias=ccol(C_BPW1 + 8 + c), scale=1.0)),
                        waits=[pk, st_free[s]])
                    PS.release(bg, a)
                    if ti < 4:
                        dd = P.emit("dve", (lambda e, s=s, ba=ba, c=c, t0=t0: e.scalar_tensor_tensor(
                            out=HB[:, c, 30 + t0:30 + t0 + 512], in0=psum[:, ba, :], scalar=ccol(C_BPW1 + c), in1=ST[:, s, :],
                            op0=ALU.add, op1=ALU.mult)), waits=[a, pk, zp])
                        if ti == 3:
                            dd = P.emit("dve", (lambda e, s=s, ba=ba, c=c: e.scalar_tensor_tensor(
                                out=VLAST[:, c, :], in0=psum[:, ba, 482:512], scalar=ccol(C_BPW1 + c), in1=ST[:, s, 482:512],
                                op0=ALU.add, op1=ALU.mult)))
                        v_tk = dd
                        vp_tk[c] = dd
                    else:
                        dd = P.emit("dve", (lambda e, s=s, ba=ba, c=c: e.scalar_tensor_tensor(
                            out=VS[:, c, :], in0=psum[:, ba, 0:NSAMP], scalar=ccol(C_BPW1 + c), in1=ST[:, s, 0:NSAMP],
                            op0=ALU.add, op1=ALU.mult)), waits=[a, pk])
                    PS.release(ba, dd)
                    st_free[s] = dd
                    last_pe = pk
                    if it_i in (1, 2) or (it_i == 0 and len(deferred) > 1):
                        flush_deferred()
                    if ti == 4:
                        sample_conv(c, dd)
            W.release(ua, last_pe)
            W.release(ug, last_pe, steal=(cp == 3))
        out_tks.append(P.emit("sp", lambda e: e.dma_start(out=nap.rearrange("(c p) k -> p c k", p=128), in_=VLAST),
                              waits=[v_tk], sem="out", amount=16))

        cb_free = {}
        q_free = [None, None]
        mv_free = [None]
        items = [(ti, c) for ti in range(4) for c in range(KC)]

        def src(ti, c, w):
            if ti == 4:
                return CSB[:, c, :]
            if c < 2 and ti % 2 == 1:
                return ST[:, c, 0:w]
            return CB[:, c, 0:w]

        def srckey(ti, c):
            return ("s", c) if ti == 4 else (("alt", c) if (c < 2 and ti % 2 == 1) else ("cb", c))

        qs_free = [None, None]

        def qbufs(ti):
            return (CQ_S, CBB_S, qs_free) if ti == 4 else (CQ, CBB, q_free)

        def stat_ops(ti, c, cev):
            w = TILES[ti][1]
            q = c % 2
            cq, cbb, qf = qbufs(ti)
            aq = P.emit("act", (lambda e, q=q, w=w, ti=ti, c=c, cq=cq: e.activation(out=cq[:, q, 0:w], in_=src(ti, c, w), func=AF.Square)),
                        waits=[cev, qf[q]])
            dq = P.emit("dve", (lambda e, q=q, w=w, ti=ti, c=c, cbb=cbb: e.tensor_copy(out=cbb[:, q, 0:w], in_=src(ti, c, w))),
                        waits=[cev, qf[q]])
            return (aq, dq)

        def stat_mm(ti, c, tks, banks):
            w = TILES[ti][1]
            q = c % 2
            s1b, s2b = banks
            if c == 0:
                PS.wait_free(s1b)
                PS.wait_free(s2b)
            P.wait("pe", tks[0])
            P.wait("pe", tks[1])
            cq, cbb, qf = qbufs(ti)
            P.emit("pe", (lambda e, c=c, q=q, w=w, s1b=s1b, cbb=cbb: e.matmul(psum[:, s1b, 0:w], ONES, cbb[:, q, 0:w],
                                                                              start=(c == 0), stop=(c == KC - 1))), sem=None)
            t = P.emit("pe", (lambda e, c=c, q=q, w=w, s2b=s2b, cq=cq: e.matmul(psum[:, s2b, 0:w], ONES, cq[:, q, 0:w],
                                                                                start=(c == 0), stop=(c == KC - 1))))
            qf[q] = t
            return t

        def chain(ti, st_last, banks):
            w = TILES[ti][1]
            s1b, s2b = banks
            m1 = P.emit("dve", (lambda e, w=w, s1b=s1b: e.tensor_scalar(out=MV[:, 0, 0:w], in0=psum[:, s1b, 0:w], scalar1=1.0 / D,
                                                                          scalar2=None, op0=ALU.mult)), waits=[st_last, mv_free[0]])
            P.wait("dve", m1)
            m2 = P.emit("dve", (lambda e, w=w: e.tensor_tensor(out=MV[:, 1, 0:w], in0=MV[:, 0, 0:w], in1=MV[:, 0, 0:w], op=ALU.mult)))
            P.wait("dve", m2)
            m3 = P.emit("dve", (lambda e, w=w, s2b=s2b: e.scalar_tensor_tensor(out=MV[:, 1, 0:w], in0=psum[:, s2b, 0:w], scalar=1.0 / D,
                                                                                 in1=MV[:, 1, 0:w], op0=ALU.mult, op1=ALU.subtract)))
            PS.release(s1b, m1)
            PS.release(s2b, m3)
            a3 = P.emit("act", (lambda e, w=w: e.activation(out=RS[:, 0, 0:w], in_=MV[:, 1, 0:w], func=AF.Ln, scale=1.0,
                                                            bias=ccol(C_EPS_LN))), waits=[m3, rs_free[0]])
            P.wait("act", a3)
            d3 = P.emit("act", (lambda e, w=w: e.activation(out=RS[:, 1, 0:w], in_=RS[:, 0, 0:w], func=AF.Exp, scale=-0.5)))
            return d3

        def apply(ti, c, d3):
            t0, w = TILES[ti]
            z1 = P.emit("dve", (lambda e, c=c, w=w, ti=ti: e.tensor_tensor(out=src(ti, c, w), in0=src(ti, c, w), in1=MV[:, 0, 0:w],
                                                                            op=ALU.subtract)), waits=[d3])
            P.wait("dve", z1)
            zz = P.emit("dve", (lambda e, c=c, w=w, ti=ti: e.tensor_tensor(out=src(ti, c, w), in0=src(ti, c, w), in1=RS[:, 1, 0:w],
                                                                            op=ALU.mult)))
            az = P.emit("act", (lambda e, c=c, t0=t0, w=w, ti=ti: e.activation(
                out=XN[:, c, t0:t0 + w], in_=src(ti, c, w), func=AF.Silu, bias=ccol(C_LNB + c), scale=ccol(C_LNG + c))),
                waits=[zz])
            cb_free[srckey(ti, c)] = az
            mv_free[0] = zz
            rs_free[0] = zz
            xn_tk[ti] = az
            return az

        def sample_tile():
            banks = tuple(PS.get(2))
            st_last = None
            for c in range(KC):
                tks = stat_ops(4, c, cs_tk[0])
                st_last = stat_mm(4, c, tks, banks)
            d3 = chain(4, st_last, banks)
            for c in range(KC):
                apply(4, c, d3)

        NPE = 28
        sslot = W.stolen[0]
        ACC = WR[:, sslot, 0:2048].bitcast(F32).rearrange("p (a b) -> p a b", a=2)
        acc_free = [W.stolen[1], W.stolen[1]]
        acc_tk = [None, None]

        def dve_taps(i_):
            ti_, c_ = items[i_]
            t0_ = TILES[ti_][0]
            j_ = i_ % 2
            tk_ = P.emit("dve", (lambda e, c_=c_, t0_=t0_, j_=j_: e.tensor_scalar(
                out=ACC[:, j_, :], in0=HB[:, c_, t0_ + NPE:t0_ + NPE + 512], scalar1=WDW[:, c_, NPE:NPE + 1], scalar2=None,
                op0=ALU.mult)), waits=[acc_free[j_], vp_tk[c_]])
            for k_ in range(NPE + 1, CA_W):
                P.wait("dve", tk_)
                tk_ = P.emit("dve", (lambda e, c_=c_, t0_=t0_, j_=j_, k_=k_: e.scalar_tensor_tensor(
                    out=ACC[:, j_, :], in0=HB[:, c_, t0_ + k_:t0_ + k_ + 512], scalar=WDW[:, c_, k_:k_ + 1], in1=ACC[:, j_, :],
                    op0=ALU.mult, op1=ALU.add)))
            acc_tk[j_] = tk_

        dve_taps(0)
        pending = None
        apply_q = []
        d3_of = {}
        for sl_ in range(2):
            P.wait("act", st_free[sl_])
            P.wait("dve", st_free[sl_])
        chain_pending = None
        for i, (ti, c) in enumerate(items):
            t0 = TILES[ti][0]
            (b,) = PS.get(1)
            PS.wait_free(b)
            P.wait("pe", dgb_tk)
            P.wait("pe", vp_tk[c])
            pk = None
            for k in range(NPE):
                for g in range(4):
                    last = (k == NPE - 1 and g == 3)
                    pk = P.emit("pe", (lambda e, k=k, g=g, c=c, t0=t0, b=b: e.matmul(
                        psum[32 * g:32 * g + 32, b, :], DGB[32 * g:32 * g + 32, c, k, :],
                        HB[32 * g:32 * g + 32, c, t0 + k:t0 + k + 512],
                        start=(k == 0), stop=(k == NPE - 1), tile_position=(32 * g, 32 * g))),
                        sem="auto" if last else None)
            last_pe = pk
            if i + 1 < len(items):
                dve_taps(i + 1)
            if i == 1:
                sample_tile()
            if c == 1 and chain_pending is not None:
                cti, cst = chain_pending
                chain_pending = None
                d3_of[cti] = chain(cti, cst, (6, 7))
                apply_q.extend((cti, cc) for cc in range(KC))
            if apply_q and c >= 2:
                for _ in range(3 if c == 2 else 1):
                    ati, ac = apply_q.pop(0)
                    apply(ati, ac, d3_of[ati])
            j = i % 2
            a = P.emit("dve", (lambda e, c=c, b=b, ti=ti, j=j: e.scalar_tensor_tensor(
                out=src(ti, c, 512), in0=psum[:, b, :], scalar=ccol(C_BDW + c), in1=ACC[:, j, :],
                op0=ALU.add, op1=ALU.add)),
                waits=[pk, cb_free.get(srckey(ti, c)), acc_tk[j]])
            PS.release(b, a)
            acc_free[j] = a
            tks = stat_ops(ti, c, a)
            if pending is not None:
                pti, pc, ptks = pending
                st_last = stat_mm(pti, pc, ptks, (6, 7))
                if pc == KC - 1:
                    chain_pending = (pti, st_last)
            pending = (ti, c, tks)
        pti, pc, ptks = pending
        st_last_f = stat_mm(pti, pc, ptks, (6, 7))
        W.give_back(acc_free[(len(items) - 1) % 2])
        W.enable_extra(last_pe)

        def m2_chain():
            P.wait("dve", sq_free[0])
            d3_of[pti] = chain(pti, st_last_f, (6, 7))

        tail_state = {"c": 0}

        def m2_tail_step():
            c_ = tail_state["c"]
            if c_ >= KC:
                return
            apply(pti, c_, d3_of[pti])
            tail_state["c"] = c_ + 1
            if c_ == KC - 1:
                st_free[0] = cb_free.get(("alt", 0))
                st_free[1] = cb_free.get(("alt", 1))
                sq_guard[0] = mv_free[0]

        def m2_tail():
            while tail_state["c"] < KC:
                m2_tail_step()

        us = [W.acquire("pw2_%d" % mp) for mp in range(4)]

        def tile_fn(ti):
            nonlocal last_pe
            t0, w = TILES[ti]
            for m in range(KC):
                _, w2v, lt = us[m // 2]
                mm = m % 2
                (b,) = PS.get(1)
                PS.wait_free(b)
                P.wait("pe", xn_tk[ti])
                P.wait("pe", lt)
                pk = None
                for k in range(KC):
                    pk = P.emit("pe", (lambda e, k=k, mm=mm, t0=t0, w=w, b=b, w2v=w2v: e.matmul(
                        psum[:, b, 0:w], w2v[:, k, mm * 128:(mm + 1) * 128], XN[:, k, t0:t0 + w],
                        start=(k == 0), stop=(k == KC - 1))), sem="auto" if k == KC - 1 else None)
                dd = P.emit("dve", (lambda e, m=m, t0=t0, w=w, b=b: e.scalar_tensor_tensor(
                    out=X[:, m, t0:t0 + w], in0=psum[:, b, 0:w], scalar=ccol(C_BPW2 + m), in1=X[:, m, t0:t0 + w],
                    op0=ALU.add, op1=ALU.add)), waits=[x_w.get((m, ti), x_ld[ti]), pk])
                PS.release(b, dd)
                x_w[(m, ti)] = dd
                x_tk[ti] = dd
                last_pe = pk
                if ti == 0:
                    m2_tail_step()
        if nxt is not None:
            nxt.alt_sq = True
        tiled_final_stage(tile_fn, nxt, hooks={0: m2_chain, 1: m2_tail})
        if nxt is not None:
            nxt.alt_sq = False
        for (u2, _, _) in us:
            W.release(u2, last_pe)
        return last_pe

    TA_s = view(O_TS, [NSAMP], F32)
    TB_s = view(O_TS + 64, [NSAMP], F32)

    def mixer_b(prev_pe, nxt):
        WCB = CONST[:, C_WCB:C_WCB + KC * CB_W].rearrange("p (c k) -> p c k", c=KC)
        sb_tk = P.emit("sp", lambda e: e.dma_start(out=SBI, in_=sB.rearrange("(c p) s -> p c s", p=128)),
                       waits=[prev_pe], sem="ld_sb", amount=16)
        SBI4 = SBI.rearrange("p c (s k) -> p c s k", k=2)
        SBO4 = SBO.rearrange("p c (s k) -> p c s k", k=2)
        last_pe = None
        y_tk = [None] * 5
        ub_free = [None, None]
        t_free = [None]
        tb_free = [None]
        tc_free = [None]
        hc_prev = [None]
        td_free = [None, None]
        td_rr = [0]
        for cp in range(4):
            uc_, wcv, ltc = W.acquire("binc%d" % cp)
            uh_, whv, lth = W.acquire("binh%d" % cp)
            ub_, wbv, ltb = W.acquire("binb%d" % cp)
            for cc in range(2):
                c = cp * 2 + cc
                for ti, (t0, w) in enumerate(TILES):
                    bb, bc, bh = PS.get(3)
                    for b_ in (bb, bc, bh):
                        PS.wait_free(b_)
                    P.wait("pe", xn_tk[ti])
                    pk = None
                    for (bk, wv, lt) in ((bc, wcv, ltc), (bh, whv, lth), (bb, wbv, ltb)):
                        P.wait("pe", lt)
                        for k in range(KC):
                            pk = P.emit("pe", (lambda e, k=k, cc=cc, t0=t0, w=w, bk=bk, wv=wv: e.matmul(
                                psum[:, bk, 0:w], wv[:, k, cc * 128:(cc + 1) * 128], XN[:, k, t0:t0 + w],
                                start=(k == 0), stop=(k == KC - 1))),
                                sem="auto" if (k == KC - 1) else None)
                        if bk == bc:
                            pkc = pk
                        elif bk == bh:
                            pkh = pk
                    s = st_next()
                    a = P.emit("act", (lambda e, s=s, w=w, bc=bc: e.activation(out=ST[:, s, 0:w], in_=psum[:, bc, 0:w], func=AF.Copy)),
                               waits=[pkc, st_free[s]])
                    PS.release(bc, a)
                    di = td_rr[0]
                    td_rr[0] ^= 1
                    ab = P.emit("act", (lambda e, di=di, w=w, bb=bb: e.activation(out=TD[:, di, 0:w], in_=psum[:, bb, 0:w], func=AF.Copy)),
                                waits=[pk, td_free[di]])
                    PS.release(bb, ab)
                    if ti < 4:
                        ui = ti % 2
                        hz = None
                        if ti == 0:
                            hz = P.emit("dve", (lambda e: e.memset(UB[:, 0, 0:2], 0.0)), waits=[ub_free[0]])
                        du = P.emit("dve", (lambda e, s=s, ui=ui, bh=bh: e.tensor_tensor(
                            out=UB[:, ui, 2:514], in0=ST[:, s, :], in1=psum[:, bh, :], op=ALU.mult)),
                            waits=[a, pkh, ub_free[ui]])
                        PS.release(bh, du)
                        st_free[s] = du
                        if ti < 3:
                            hc = P.emit("act", (lambda e, ui=ui: e.activation(out=UB[:, 1 - ui, 0:2], in_=UB[:, ui, 512:514], func=AF.Copy)),
                                        waits=[du, ub_free[1 - ui]])
                        else:
                            hc = P.emit("act", (lambda e, ui=ui, c=c: e.activation(out=UBL[:, c, :], in_=UB[:, ui, 512:514], func=AF.Copy)),
                                        waits=[du])
                        s1 = P.emit("act", (lambda e, ui=ui, c=c: e.activation(
                            out=TA, in_=UB[:, ui, 2:514], func=AF.Copy, scale=WCB[:, c, 2:3])),
                            waits=[du, t_free[0]])
                        t2 = P.emit("dve", (lambda e, ui=ui, c=c: e.scalar_tensor_tensor(
                            out=TB, in0=UB[:, ui, 1:513], scalar=WCB[:, c, 1:2], in1=TA, op0=ALU.mult, op1=ALU.add)),
                            waits=[s1, hc_prev[0], hz])
                        P.wait("dve", t2)
                        t3 = P.emit("dve", (lambda e, ui=ui, c=c: e.scalar_tensor_tensor(
                            out=TA, in0=UB[:, ui, 0:512], scalar=WCB[:, c, 0:1], in1=TB, op0=ALU.mult, op1=ALU.add)))
                        P.wait("dve", t3)
                        hc_prev[0] = hc
                        ub_free[ui] = t3
                        yy = P.emit("dve", (lambda e, c=c, t0=t0, di=di: e.tensor_tensor(
                            out=HB[:, c, t0:t0 + 512], in0=TA, in1=TD[:, di, :], op=ALU.mult)), waits=[ab])
                        t_free[0] = yy
                        td_free[di] = yy
                    else:
                        du = P.emit("dve", (lambda e, s=s, c=c, bh=bh: e.tensor_tensor(
                            out=US[:, c, :], in0=ST[:, s, 0:NSAMP], in1=psum[:, bh, 0:NSAMP], op=ALU.mult)),
                            waits=[a, pkh, sb_tk])
                        PS.release(bh, du)
                        st_free[s] = du
                        s1 = P.emit("act", (lambda e, c=c: e.activation(
                            out=TA[:, 0:NSAMP], in_=US[:, c, :], func=AF.Copy, scale=WCB[:, c, 2:3])),
                            waits=[du, t_free[0], sb_tk])
                        t2 = P.emit("dve", (lambda e, c=c: e.scalar_tensor_tensor(
                            out=TB[:, 0:NSAMP], in0=SBI4[:, c, :, 1], scalar=WCB[:, c, 1:2], in1=TA[:, 0:NSAMP],
                            op0=ALU.mult, op1=ALU.add)), waits=[s1])
                        P.wait("dve", t2)
                        t3 = P.emit("dve", (lambda e, c=c: e.scalar_tensor_tensor(
                            out=TA[:, 0:NSAMP], in0=SBI4[:, c, :, 0], scalar=WCB[:, c, 0:1], in1=TB[:, 0:NSAMP],
                            op0=ALU.mult, op1=ALU.add)))
                        P.wait("dve", t3)
                        yy = P.emit("dve", (lambda e, c=c, t0=t0, di=di: e.tensor_tensor(
                            out=HB[:, c, t0:t0 + NSAMP], in0=TA[:, 0:NSAMP], in1=TD[:, di, 0:NSAMP], op=ALU.mult)), waits=[ab])
                        t_free[0] = yy
                        td_free[di] = yy
                        P.emit("act", (lambda e, c=c: e.activation(out=SBO4[:, c, :, 0], in_=SBI4[:, c, :, 1], func=AF.Copy)),
                               waits=[sb_tk])
                        sbo_tk = P.emit("act", (lambda e, c=c: e.activation(out=SBO4[:, c, :, 1], in_=US[:, c, :], func=AF.Copy)),
                                        waits=[du])
                        y_tk[ti] = sbo_tk
                    xn_tk_b[ti] = yy
                    last_pe = pk
                    if ti in (0, 1):
                        flush_deferred()
            W.release(uc_, last_pe)
            W.release(uh_, last_pe)
            W.release(ub_, last_pe)
        out_tks.append(P.emit("sp", lambda e: e.dma_start(out=nbp.rearrange("(c p) k -> p c k", p=128), in_=UBL),
                              waits=[xn_tk_b[3], ("p_act", P.cnt["p_act"])], sem="out", amount=16))
        out_tks.append(P.emit("sp", lambda e: e.dma_start(out=nbs.rearrange("(c p) s -> p c s", p=128), in_=SBO),
                              waits=[y_tk[4]], sem="out", amount=16))
        us = [W.acquire("bout%d" % mp) for mp in range(4)]

        def tile_fn(ti):
            nonlocal last_pe
            t0, w = TILES[ti]
            for m in range(KC):
                _, wov, lt = us[m // 2]
                mm = m % 2
                (b,) = PS.get(1)
                PS.wait_free(b)
                P.wait("pe", xn_tk_b[ti])
                P.wait("pe", lt)
                pk = None
                for k in range(KC):
                    pk = P.emit("pe", (lambda e, k=k, mm=mm, t0=t0, w=w, b=b, wov=wov: e.matmul(
                        psum[:, b, 0:w], wov[:, k, mm * 128:(mm + 1) * 128], HB[:, k, t0:t0 + w],
                        start=(k == 0), stop=(k == KC - 1))), sem="auto" if k == KC - 1 else None)
                dd = P.emit("dve", (lambda e, m=m, t0=t0, w=w, b=b: e.tensor_tensor(
                    out=X[:, m, t0:t0 + w], in0=psum[:, b, 0:w], in1=X[:, m, t0:t0 + w], op=ALU.add)),
                    waits=[x_w.get((m, ti), x_ld[ti]), pk])
                PS.release(b, dd)
                x_w[(m, ti)] = dd
                x_tk[ti] = dd
                last_pe = pk
        tiled_final_stage(tile_fn, nxt, order=[0, 4, 1, 2, 3])
        for (uo, _, _) in us:
            W.release(uo, last_pe)
        return last_pe

    xn_tk_b = [None] * 5
    ydst = yT.rearrange("(c p) t -> p c t", p=128)

    norms = [Norm(C_FFN1N[0]), Norm(C_MIXN[0]), Norm(C_FFN2N[0]), Norm(C_FFN1N[1]), Norm(C_MIXN[1]), Norm(C_FFN2N[1]),
             Norm(C_FINN, final=True)]

    def nx(i):
        return norms[i + 1] if i + 1 <= stop_after else None

    last = None
    for ti_ in (0, 1):
        norms[0].part1(ti_)
        norms[0].part2(ti_)
    norms[0].override = {3: XN[:, :, 1536:2048]}
    for ti_ in (2, 3, 4):
        norms[0].part1a(ti_)
        deferred.append((norms[0], ti_, False))
    if stop_after >= 0:
        last = ffn("f10", nx(0))
    if stop_after >= 1:
        last = mixer_a(last, nx(1))
    if stop_after >= 2:
        last = ffn("f20", nx(2))
    if stop_after >= 3:
        last = ffn("f11", nx(3))
    if stop_after >= 4:
        last = mixer_b(last, nx(4))
    if stop_after >= 5:
        last = ffn("f21", nx(5))
    while deferred:
        flush_deferred()
    if stop_after < 6:
        for ti, (t0, w) in enumerate(TILES):
            out_tks.append(P.emit("sp", (lambda e, t0=t0, w=w: e.dma_start(out=ydst[:, :, t0:t0 + w], in_=X[:, :, t0:t0 + w])),
                                  waits=[x_tk[ti]], sem="out", amount=16))
    for sem in ("out", "so"):
        if sem in P.cnt:
            P.wait("sp", (sem, P.cnt[sem]))
    assert W.cursor == len(W.units) and W.next_dma == len(W.units)


    sem_names = sorted(P.cnt.keys())
    sems = {n: es.enter_context(nc.semaphore(n)) for n in sem_names}
    handles = {"pe": None, "act": None, "dve": None, "pool": None, "sp": None}

    def replay(name, eng):
        fuse = name in ("pe", "act", "dve")
        ops = P.ops[name]
        pend = []
        for op in ops:
            if op[0] == "wait":
                pend.append(op)
                continue
            fused = None
            if fuse and pend:
                fused = pend.pop()
            for w_ in pend:
                eng.wait_ge(sems[w_[1]], w_[2])
            pend = []
            ins = op[1](eng)
            if fused is not None:
                ins._wait_ge(sems[fused[1]], fused[2])
            if op[2] is not None:
                ins.then_inc(sems[op[2][0]], op[2][1])
        for w_ in pend:
            eng.wait_ge(sems[w_[1]], w_[2])

    with nc.Block() as block:
        @block.tensor
        def _(e):
            replay("pe", e)

        @block.scalar
        def _(e):
            replay("act", e)

        @block.vector
        def _(e):
            replay("dve", e)

        @block.gpsimd
        def _(e):
            replay("pool", e)

        @block.sync
        def _(e):
            replay("sp", e)
    es.close()
    return nc


def _pack_consts(inp):
    f = np.float32

    def fm(v):
        return np.asarray(v, f).reshape(KC, 128).T

    c = np.zeros((128, NCONST), f)
    for l in range(2):
        c[:, C_FFN1N[l]:C_FFN1N[l] + 8] = fm(inp["ffn1_norm"][l])
        c[:, C_MIXN[l]:C_MIXN[l] + 8] = fm(inp["mix_norm"][l])
        c[:, C_FFN2N[l]:C_FFN2N[l] + 8] = fm(inp["ffn2_norm"][l])
    c[:, C_FINN:C_FINN + 8] = fm(inp["final_norm"])
    c[:, C_BPW1:C_BPW1 + 16] = np.asarray(inp["a_b_pw1"][0], f).reshape(16, 128).T
    c[:, C_BDW:C_BDW + 8] = fm(inp["a_b_dw"][0])
    c[:, C_LNG:C_LNG + 8] = fm(inp["a_ln_g"][0])
    c[:, C_LNB:C_LNB + 8] = fm(inp["a_ln_b"][0])
    c[:, C_BPW2:C_BPW2 + 8] = fm(inp["a_b_pw2"][0])
    wdw = np.asarray(inp["a_w_dw"][0], f)
    c[:, C_WDW:C_WDW + KC * CA_W] = wdw.reshape(CA_W, KC, 128).transpose(2, 1, 0).reshape(128, KC * CA_W)
    wcb = np.asarray(inp["b_w_conv"][0], f)
    c[:, C_WCB:C_WCB + KC * CB_W] = wcb.reshape(CB_W, KC, 128).transpose(2, 1, 0).reshape(128, KC * CB_W)
    c[:, C_ID32:C_ID32 + 32] = (np.arange(32)[None, :] == (np.arange(128) % 32)[:, None]).astype(f)
    c[:, C_EPS_RMS] = RMS_EPS
    c[:, C_EPS_LN] = LN_EPS
    return c


_NC_CACHE = {}


def _run(inp, stop_after=6, trace=False):
    if stop_after not in _NC_CACHE:
        _NC_CACHE[stop_after] = build_program(stop_after)
    nc = _NC_CACHE[stop_after]
    f = np.float32
    consts = _pack_consts(inp)
    xp = np.asarray(inp["x_prompt"], f)
    xs = np.asarray(inp["x_sample"], f)[:, 0, :]
    sa = np.asarray(inp["state_conv_a"], f)[0]
    sb = np.asarray(inp["state_conv_b"], f)[0]
    shared = {k: np.ascontiguousarray(np.asarray(inp[k], f)) for k in
              ("ffn1_w_gate", "ffn1_w_up", "ffn1_w_down", "ffn2_w_gate", "ffn2_w_up", "ffn2_w_down")}
    shared["a_w_pw1"] = np.ascontiguousarray(np.asarray(inp["a_w_pw1"], f)[0])
    shared["a_w_pw2"] = np.ascontiguousarray(np.asarray(inp["a_w_pw2"], f)[0])
    shared["b_w_in"] = np.ascontiguousarray(np.asarray(inp["b_w_in"], f)[0])
    shared["b_w_out"] = np.ascontiguousarray(np.asarray(inp["b_w_out"], f)[0])
    in_maps = []
    for b in range(NCORES):
        sl = slice(b * NSAMP, (b + 1) * NSAMP)
        xt = np.empty((D, TT), f)
        xt[:, :SEQ] = xp[b].T
        xt[:, SEQ:] = xs[sl].T
        m = dict(shared)
        m["xT"] = xt
        m["consts"] = consts
        m["sA"] = np.ascontiguousarray(sa[sl].transpose(2, 0, 1)).reshape(D, NSAMP * 30)
        m["sB"] = np.ascontiguousarray(sb[sl].transpose(2, 0, 1)).reshape(D, NSAMP * 2)
        in_maps.append(m)
    res = run_bass_kernel_spmd(nc, in_maps, core_ids=list(range(NCORES)), trace=trace)
    return res


def kernel(**inputs):
    res = _run(inputs)
    f = np.float32
    y_prompt = np.empty((NCORES, SEQ, D), f)
    y_sample = np.empty((NCORES * NSAMP, 1, D), f)
    na_p = np.empty((1, NCORES, 30, D), f)
    na_s = np.empty((1, NCORES * NSAMP, 30, D), f)
    nb_p = np.empty((1, NCORES, 2, D), f)
    nb_s = np.empty((1, NCORES * NSAMP, 2, D), f)
    for b, r in enumerate(res.results):
        sl = slice(b * NSAMP, (b + 1) * NSAMP)
        yt = np.asarray(r["yT"], f)
        y_prompt[b] = yt[:, :SEQ].T
        y_sample[sl, 0, :] = yt[:, SEQ:].T
        na_p[0, b] = np.asarray(r["nap"], f).T
        na_s[0, sl] = np.asarray(r["nas"], f).reshape(D, NSAMP, 30).transpose(1, 2, 0)
        nb_p[0, b] = np.asarray(r["nbp"], f).T
        nb_s[0, sl] = np.asarray(r["nbs"], f).reshape(D, NSAMP, 2).transpose(1, 2, 0)
    return (y_prompt, y_sample, na_p, na_s, nb_p, nb_s)
```

```python
import contextlib
import numpy as np
import concourse.bass as bass
import concourse.mybir as mybir
from concourse.bass_utils import run_bass_kernel_spmd

F32 = mybir.dt.float32
BF16 = mybir.dt.bfloat16
U8 = mybir.dt.uint8
AF = mybir.ActivationFunctionType
ALU = mybir.AluOpType
AX = mybir.AxisListType

NCORES = 8
D = 1024
DFF = 2816
SEQ = 2048
NSAMP = 16
TT = SEQ + NSAMP
KC = 8
CA_W = 31
CB_W = 3
TILES = [(0, 512), (512, 512), (1024, 512), (1536, 512), (2048, 16)]
FFN_GROUPS = [(0, 6), (6, 8), (14, 8)]
RMS_EPS = 1e-6
LN_EPS = 1e-5

C_FFN1N = (0, 8)
C_MIXN = (16, 24)
C_FFN2N = (32, 40)
C_FINN = 48
C_BPW1 = 56
C_BDW = 72
C_LNG = 80
C_LNB = 88
C_BPW2 = 96
C_WDW = 104
C_WCB = 352
C_ID32 = 376
C_EPS_RMS = 408
C_EPS_LN = 409
NCONST = 416

NSLOT = 5
SLOT_ELEMS = 2048
O_X = 0
O_XN = O_X + KC * TT * 4
O_HB = O_XN + KC * TT * 2
HB_W = 2080
O_WR = O_HB + KC * HB_W * 2
O_CONST = O_WR + NSLOT * SLOT_ELEMS * 2
O_ONES = O_CONST + NCONST * 4
O_SCR = O_ONES + 256
O_ST = O_SCR
O_RS = O_ST + 4096
O_SQ = O_RS + 4096
O_CB = O_SQ + 8192
O_DG = O_CB + 16384
O_SM = O_DG + 15872
O_VLAST = O_SM
O_VS = O_VLAST + 960
O_SA = O_VS + 512
O_SO = O_SA + 3840
O_TS = O_SO + 1920
O_CSB = O_TS + 128
O_QS = O_CSB + 512
ARENA = O_QS + 128


class Prog:
    ENGS = ("pe", "act", "dve", "pool", "sp")

    def __init__(self):
        self.ops = {e: [] for e in self.ENGS}
        self.cnt = {}
        self.waited = {e: {} for e in self.ENGS}

    def tick(self, sem, amount):
        self.cnt[sem] = self.cnt.get(sem, 0) + amount
        return (sem, self.cnt[sem])

    def wait(self, eng, tk):
        if tk is None:
            return
        sem, val = tk
        if self.waited[eng].get(sem, 0) >= val:
            return
        self.waited[eng][sem] = val
        self.ops[eng].append(("wait", sem, val))

    def emit(self, eng, fn, waits=(), sem="auto", amount=1):
        for w in waits:
            self.wait(eng, w)
        if sem == "auto":
            sem = "p_" + eng
        tk = None
        if sem is not None:
            tk = self.tick(sem, amount)
        self.ops[eng].append(("op", fn, (sem, amount) if sem is not None else None))
        return tk


def build_program(stop_after=6):
    nc = bass.Bass("TRN2", target_bir_lowering=False)
    P = Prog()

    def din(name, shape):
        return nc.dram_tensor(name, list(shape), F32, kind="ExternalInput").ap()

    def dout(name, shape):
        return nc.dram_tensor(name, list(shape), F32, kind="ExternalOutput").ap()

    xT = din("xT", [D, TT])
    consts = din("consts", [128, NCONST])
    sA = din("sA", [D, NSAMP * 30])
    sB = din("sB", [D, NSAMP * 2])
    w_f1g = din("ffn1_w_gate", [2, D, DFF])
    w_f1u = din("ffn1_w_up", [2, D, DFF])
    w_f1d = din("ffn1_w_down", [2, DFF, D])
    w_f2g = din("ffn2_w_gate", [2, D, DFF])
    w_f2u = din("ffn2_w_up", [2, D, DFF])
    w_f2d = din("ffn2_w_down", [2, DFF, D])
    w_pw1 = din("a_w_pw1", [D, 2 * D])
    w_pw2 = din("a_w_pw2", [D, D])
    w_bin = din("b_w_in", [D, 3 * D])
    w_bout = din("b_w_out", [D, D])
    yT = dout("yT", [D, TT])
    nap = dout("nap", [D, 30])
    nas = dout("nas", [D, NSAMP * 30])
    nbp = dout("nbp", [D, 2])
    nbs = dout("nbs", [D, NSAMP * 2])

    es = contextlib.ExitStack()
    arena = es.enter_context(nc.sbuf_tensor("arena", [128, ARENA], U8))
    psum = es.enter_context(nc.psum_tensor("ps", [128, 8, 512], F32))

    def view(off, shape, dt):
        isz = 4 if dt == F32 else 2
        n = int(np.prod(shape))
        v = arena[:, off:off + n * isz].bitcast(dt)
        if len(shape) == 1:
            return v
        if len(shape) == 2:
            return v.rearrange("p (a b) -> p a b", a=shape[0])
        if len(shape) == 3:
            return v.rearrange("p (a b c) -> p a b c", a=shape[0], b=shape[1])
        raise ValueError

    X = view(O_X, [KC, TT], F32)
    XN = view(O_XN, [KC, TT], BF16)
    HB = view(O_HB, [KC, HB_W], BF16)
    WR = view(O_WR, [NSLOT, SLOT_ELEMS], BF16)
    CONST = view(O_CONST, [NCONST], F32)
    ONES = view(O_ONES, [128], BF16)
    ST = view(O_ST, [2, 512], F32)
    RS = view(O_RS, [2, 512], F32)
    SQ = view(O_SQ, [KC, 512], BF16)
    CB = view(O_CB, [KC, 512], F32)
    DGB = view(O_DG, [KC, CA_W, 32], BF16)
    VLAST = view(O_VLAST, [KC, 30], F32)
    VS = view(O_VS, [KC, NSAMP], F32)
    SA = view(O_SA, [2, NSAMP, 30], F32)
    SO = view(O_SO, [NSAMP, 30], F32)
    CSB = view(O_CSB, [KC, NSAMP], F32)
    SQ_S = view(O_CSB, [KC, NSAMP], BF16)
    CBB_S = view(O_QS, [2, NSAMP], BF16)
    CQ_S = view(O_QS + 64, [2, NSAMP], BF16)
    UB = view(O_CB, [2, 514], F32)
    TA = view(O_CB + 4160, [512], F32)
    TB = view(O_CB + 6208, [512], F32)
    US = view(O_CB + 10304, [KC, NSAMP], F32)
    UBL = view(O_CB + 10816, [KC, 2], F32)
    SBO = view(O_CB + 10880, [KC, NSAMP * 2], F32)
    TD = view(O_CB + 12288, [2, 512], F32)
    SBI = view(O_CB + 8256, [KC, NSAMP * 2], F32)
    CBB = view(O_SQ, [2, 512], BF16)
    CQ = view(O_SQ + 2048, [2, 512], BF16)
    MV = view(O_SQ + 4096, [2, 512], F32)

    def ccol(c0, n=1):
        return CONST[:, c0:c0 + n]

    class Banks:
        def __init__(self):
            self.free = [[] for _ in range(8)]
            self.rr = 0
            self.srr = 0

        def get(self, n):
            out = []
            for _ in range(n):
                out.append(self.rr)
                self.rr = (self.rr + 1) % 6
            return out

        def stat(self):
            b = 6 + self.srr
            self.srr ^= 1
            return b

        def wait_free(self, b):
            for tk in self.free[b]:
                P.wait("pe", tk)
            self.free[b] = []

        def release(self, b, *tks):
            self.free[b] = [t for t in tks if t is not None]

    PS = Banks()

    NEXTRA = 3
    WRX = view(O_DG, [NEXTRA, SLOT_ELEMS], BF16)

    def slot_view(slot, shape):
        n = int(np.prod(shape))
        base = WR[:, slot, 0:n] if slot < NSLOT else WRX[:, slot - NSLOT, 0:n]
        return base.rearrange("p (a b) -> p a b", a=shape[0])

    class WRing:
        def __init__(self):
            self.units = []
            self.next_dma = 0
            self.free_slots = [(i, None) for i in range(NSLOT)]
            self.slot_of = {}
            self.load_tk = {}
            self.cursor = 0
            self.stolen = None

        def plan(self, src, shape, tag):
            self.units.append((src, shape, tag))

        def enable_extra(self, tk):
            for i in range(NEXTRA):
                self.free_slots.append((NSLOT + i, tk))
            self.topup()

        def topup(self):
            while self.next_dma < len(self.units) and self.free_slots:
                v = self.next_dma
                slot, ftk = self.free_slots.pop(0)
                src, shape, _ = self.units[v]
                dst = slot_view(slot, shape)
                prio = ld_x[0] if v == 0 else (ld_x[4] if v == 2 else None)
                tk = P.emit("pool", (lambda e, dst=dst, src=src: e.dma_start(out=dst, in_=src)),
                            waits=[ftk, prio], sem="w%d" % slot, amount=16)
                self.load_tk[v] = tk
                self.slot_of[v] = slot
                self.next_dma += 1

        def acquire(self, tag):
            u = self.cursor
            assert self.units[u][2] == tag, (self.units[u][2], tag)
            self.cursor += 1
            self.topup()
            assert u in self.load_tk, "weight ring exhausted (too many units held)"
            return u, slot_view(self.slot_of[u], self.units[u][1]), self.load_tk[u]

        def release(self, u, tk, steal=False):
            if steal:
                self.stolen = (self.slot_of[u], tk)
                return
            self.free_slots.append((self.slot_of[u], tk))
            self.topup()

        def give_back(self, tk):
            self.free_slots.append((self.stolen[0], tk))
            self.stolen = None
            self.topup()

    W = WRing()
    ld_x = []
    x_ld = []

    def colblock(wmat, c0, width=256):
        return wmat[:, c0:c0 + width].rearrange("(k p) f -> p k f", p=128)

    def rowblock(wmat, r0, nrows_chunks, c0, width=256):
        return wmat[r0 * 128:(r0 + nrows_chunks) * 128, c0:c0 + width].rearrange("(r p) f -> p r f", p=128)

    def plan_ffn(name, wg, wu, wd):
        for (j0, n) in FFN_GROUPS:
            for jp in range(0, n, 2):
                W.plan(colblock(wg, (j0 + jp) * 128), [KC, 256], name + "g%d" % (j0 + jp))
                W.plan(colblock(wu, (j0 + jp) * 128), [KC, 256], name + "u%d" % (j0 + jp))
            for mp in range(4):
                W.plan(rowblock(wd, j0, n, mp * 256), [n, 256], name + "d%d_%d" % (j0, mp))

    phases = ["ffn1_0", "mixA", "ffn2_0", "ffn1_1", "mixB", "ffn2_1", "final"]
    if stop_after >= 0:
        plan_ffn("f10", w_f1g[0], w_f1u[0], w_f1d[0])
    if stop_after >= 1:
        for cp in range(4):
            W.plan(colblock(w_pw1, cp * 256), [KC, 256], "pw1a%d" % cp)
            W.plan(colblock(w_pw1, 1024 + cp * 256), [KC, 256], "pw1g%d" % cp)
        for mp in range(4):
            W.plan(colblock(w_pw2, mp * 256), [KC, 256], "pw2_%d" % mp)
    if stop_after >= 2:
        plan_ffn("f20", w_f2g[0], w_f2u[0], w_f2d[0])
    if stop_after >= 3:
        plan_ffn("f11", w_f1g[1], w_f1u[1], w_f1d[1])
    if stop_after >= 4:
        for cp in range(4):
            W.plan(colblock(w_bin, 1024 + cp * 256), [KC, 256], "binc%d" % cp)
            W.plan(colblock(w_bin, 2048 + cp * 256), [KC, 256], "binh%d" % cp)
            W.plan(colblock(w_bin, cp * 256), [KC, 256], "binb%d" % cp)
        for mp in range(4):
            W.plan(colblock(w_bout, mp * 256), [KC, 256], "bout%d" % mp)
    if stop_after >= 5:
        plan_ffn("f21", w_f2g[1], w_f2u[1], w_f2d[1])

    x_tk = [None] * 5
    x_w = {}
    xn_tk = [None] * 5
    st_free = [None, None]
    st_rr = [0]
    rs_free = [None]
    sq_free = [None]
    sqs_free = [None]
    out_tks = []

    def st_next():
        s = st_rr[0]
        st_rr[0] ^= 1
        return s

    c_tk = P.emit("sp", lambda e: e.dma_start(out=CONST, in_=consts), sem="ld_c", amount=16)
    xsrc = xT.rearrange("(c p) t -> p c t", p=128)
    for ti, (t0, w) in enumerate(TILES):
        x_tk[ti] = P.emit("sp", (lambda e, t0=t0, w=w: e.dma_start(out=X[:, :, t0:t0 + w], in_=xsrc[:, :, t0:t0 + w])),
                          sem="ld_x%d" % ti, amount=16)
        ld_x.append(x_tk[ti])
        x_ld.append(x_tk[ti])
    ones_tk = P.emit("dve", lambda e: e.memset(ONES, 1.0))
    P.wait("act", c_tk)
    P.wait("dve", c_tk)
    P.wait("pool", c_tk)
    P.wait("pe", ones_tk)

    class Norm:
        def __init__(self, gcol, final=False):
            self.gcol = gcol
            self.final = final
            self.stat = {}
            self.sq = {}
            self.sqbuf = {}
            self.alt_sq = False
            self.override = {}

        def part1a(self, ti):
            t0, w = TILES[ti]
            if ti in self.override:
                ob = self.override[ti]
                a1 = P.emit("act", (lambda e, t0=t0, w=w, ob=ob: e.activation(out=ob[:, :, 0:w], in_=X[:, :, t0:t0 + w], func=AF.Square)),
                            waits=[x_tk[ti]])
                self.sqbuf[ti] = ob
                self.sq[ti] = a1
                return
            if ti == 4:
                a1 = P.emit("act", (lambda e, t0=t0, w=w: e.activation(out=SQ_S, in_=X[:, :, t0:t0 + w], func=AF.Square)),
                            waits=[x_tk[ti], sqs_free[0]])
            elif self.alt_sq:
                a1 = P.emit("act", (lambda e, t0=t0, w=w: e.activation(out=HB[:, :, 1536:1536 + w], in_=X[:, :, t0:t0 + w], func=AF.Square)),
                            waits=[x_tk[ti], sq_free[0]])
            else:
                a1 = P.emit("act", (lambda e, t0=t0, w=w: e.activation(out=SQ[:, :, 0:w], in_=X[:, :, t0:t0 + w], func=AF.Square)),
                            waits=[x_tk[ti], sq_free[0], sq_guard[0]])
            self.sqbuf[ti] = SQ_S if ti == 4 else (HB[:, :, 1536:2048] if self.alt_sq else SQ)
            self.sq[ti] = a1

        def part1b(self, ti):
            t0, w = TILES[ti]
            a1 = self.sq.pop(ti)
            bank = PS.stat()
            PS.wait_free(bank)
            P.wait("pe", a1)
            p1 = None
            sqb = self.sqbuf.pop(ti)
            for c in range(KC):
                p1 = P.emit("pe", (lambda e, c=c, w=w, bank=bank, sqb=sqb: e.matmul(psum[:, bank, 0:w], ONES, sqb[:, c, 0:w],
                                                                                     start=(c == 0), stop=(c == KC - 1))),
                            sem="auto" if c == KC - 1 else None)
            if ti == 4:
                sqs_free[0] = p1
            else:
                sq_free[0] = p1
            self.stat[ti] = (bank, p1)

        def part1(self, ti):
            self.part1a(ti)
            self.part1b(ti)

        def part2(self, ti):
            t0, w = TILES[ti]
            bank, p1 = self.stat.pop(ti)
            gcol = self.gcol
            a2 = P.emit("act", (lambda e, w=w, bank=bank: e.activation(out=RS[:, 0, 0:w], in_=psum[:, bank, 0:w], func=AF.Ln,
                                                                         scale=1.0 / D, bias=ccol(C_EPS_RMS))),
                        waits=[p1, rs_free[0]])
            PS.release(bank, a2)
            P.wait("act", a2)
            d1 = P.emit("act", (lambda e, w=w: e.activation(out=RS[:, 1, 0:w], in_=RS[:, 0, 0:w], func=AF.Exp, scale=-0.5)),
                        waits=[x_tk[ti]])
            P.wait("dve", d1)
            d = None
            dst = X if self.final else XN
            for c in range(KC):
                d = P.emit("dve", (lambda e, c=c, t0=t0, w=w, dst=dst: e.scalar_tensor_tensor(
                    out=dst[:, c, t0:t0 + w], in0=X[:, c, t0:t0 + w], scalar=ccol(gcol + c), in1=RS[:, 1, 0:w],
                    op0=ALU.mult, op1=ALU.mult)))
                if self.final and c % 2 == 1:
                    out_tks.append(P.emit("sp", (lambda e, c=c, t0=t0, w=w: e.dma_start(
                        out=ydst[:, c - 1:c + 1, t0:t0 + w], in_=X[:, c - 1:c + 1, t0:t0 + w])),
                        waits=[d], sem="out", amount=16))
            rs_free[0] = d
            if self.final:
                x_tk[ti] = d
            else:
                xn_tk[ti] = d

        def all(self):
            for ti in range(5):
                self.part1(ti)
                self.part2(ti)

    deferred = []
    sq_guard = [None]

    def flush_deferred():
        if deferred:
            nrm, ti, full = deferred.pop(0)
            if full:
                nrm.part1a(ti)
            nrm.part1b(ti)
            nrm.part2(ti)

    FINAL_ORDER = [4, 0, 1, 2, 3]
    NORMAL_ORDER = [(jj, ti) for jj in range(2) for ti in range(5)]
    FIRST_ORDER = [(0, 0), (0, 1), (1, 0), (1, 1), (0, 2), (0, 3), (0, 4), (1, 2), (1, 3), (1, 4)]

    def tiled_final_stage(emit_tile, nxt, hooks=None, order=None):
        prev = None
        order = order or FINAL_ORDER
        early = set()
        for idx, ti in enumerate(order):
            emit_tile(ti)
            if nxt is not None and prev is not None and prev not in early:
                nxt.part1a(prev)
            if nxt is not None and ti == 4:
                nxt.part1a(ti)
                early.add(ti)
            if hooks is not None and idx in hooks:
                hooks[idx]()
            if nxt is not None and prev is not None:
                nxt.part1b(prev)
                nxt.part2(prev)
            prev = ti
        if nxt is not None:
            if prev not in early:
                nxt.part1a(prev)
            deferred.append((nxt, prev, False))

    def ffn(name, nxt):
        last_pe = None
        for gi, (j0, n) in enumerate(FFN_GROUPS):
            h_tk = [None] * 5
            for jp in range(0, n, 2):
                ug, wgv, ltg = W.acquire(name + "g%d" % (j0 + jp))
                uu, wuv, ltu = W.acquire(name + "u%d" % (j0 + jp))
                first = (gi == 0 and jp == 0)
                for it_i, (jj, ti) in enumerate(FIRST_ORDER if first else NORMAL_ORDER):
                    if True:
                        hj = jp + jj
                        t0, w = TILES[ti]
                        bg, bu = PS.get(2)
                        P.wait("pe", xn_tk[ti])
                        P.wait("pe", ltg)
                        PS.wait_free(bg)
                        for k in range(KC):
                            P.emit("pe", (lambda e, k=k, jj=jj, t0=t0, w=w, bg=bg, wgv=wgv: e.matmul(
                                psum[:, bg, 0:w], wgv[:, k, jj * 128:(jj + 1) * 128], XN[:, k, t0:t0 + w],
                                start=(k == 0), stop=(k == KC - 1))), sem=None)
                        P.wait("pe", ltu)
                        PS.wait_free(bu)
                        for k in range(KC):
                            pk = P.emit("pe", (lambda e, k=k, jj=jj, t0=t0, w=w, bu=bu, wuv=wuv: e.matmul(
                                psum[:, bu, 0:w], wuv[:, k, jj * 128:(jj + 1) * 128], XN[:, k, t0:t0 + w],
                                start=(k == 0), stop=(k == KC - 1))), sem="auto" if k == KC - 1 else None)
                        s = st_next()
                        a = P.emit("act", (lambda e, s=s, w=w, bg=bg: e.activation(out=ST[:, s, 0:w], in_=psum[:, bg, 0:w], func=AF.Silu)),
                                   waits=[pk, st_free[s]])
                        dd = P.emit("dve", (lambda e, s=s, w=w, bu=bu, hj=hj, t0=t0: e.tensor_tensor(
                            out=HB[:, hj, t0:t0 + w], in0=ST[:, s, 0:w], in1=psum[:, bu, 0:w], op=ALU.mult)), waits=[a])
                        PS.release(bg, a)
                        PS.release(bu, dd)
                        st_free[s] = dd
                        h_tk[ti] = dd if h_tk[ti] is None or dd[1] > h_tk[ti][1] else h_tk[ti]
                        last_pe = pk
                        if it_i in (1, 2) or (it_i == 0 and len(deferred) > 1):
                            flush_deferred()
                W.release(ug, last_pe)
                W.release(uu, last_pe)

            def down_item(m, ti, wdv, lt):
                nonlocal last_pe
                t0, w = TILES[ti]
                mm = m % 2
                (b,) = PS.get(1)
                PS.wait_free(b)
                P.wait("pe", h_tk[ti])
                P.wait("pe", lt)
                pk = None
                for jj in range(n):
                    pk = P.emit("pe", (lambda e, jj=jj, mm=mm, t0=t0, w=w, b=b, wdv=wdv, n=n: e.matmul(
                        psum[:, b, 0:w], wdv[:, jj, mm * 128:(mm + 1) * 128], HB[:, jj, t0:t0 + w],
                        start=(jj == 0), stop=(jj == n - 1))), sem="auto" if jj == n - 1 else None)
                dd = P.emit("dve", (lambda e, m=m, t0=t0, w=w, b=b: e.scalar_tensor_tensor(
                    out=X[:, m, t0:t0 + w], in0=psum[:, b, 0:w], scalar=0.5, in1=X[:, m, t0:t0 + w],
                    op0=ALU.mult, op1=ALU.add)), waits=[x_w.get((m, ti), x_ld[ti]), pk])
                PS.release(b, dd)
                x_w[(m, ti)] = dd
                x_tk[ti] = dd
                last_pe = pk

            if gi < len(FFN_GROUPS) - 1:
                for mp in range(4):
                    ud, wdv, ltd = W.acquire(name + "d%d_%d" % (j0, mp))
                    for mm in range(2):
                        for ti in range(5):
                            down_item(mp * 2 + mm, ti, wdv, ltd)
                    W.release(ud, last_pe)
            else:
                us = [W.acquire(name + "d%d_%d" % (j0, mp)) for mp in range(4)]

                def tile_fn(ti):
                    for m in range(KC):
                        _, wdv, ltd = us[m // 2]
                        down_item(m, ti, wdv, ltd)
                tiled_final_stage(tile_fn, nxt)
                for (ud, _, _) in us:
                    W.release(ud, last_pe)
        return last_pe

    def mixer_a(prev_pe, nxt):
        ID32 = CONST[:, C_ID32:C_ID32 + 32]
        WDW = CONST[:, C_WDW:C_WDW + KC * CA_W].rearrange("p (c k) -> p c k", c=KC)
        zp = P.emit("dve", lambda e: e.memset(HB[:, :, 0:30], 0.0), waits=[prev_pe])
        v_tk = zp
        last_pe = None
        sa_ld = [None, None]
        sa_free = [None, None]
        so_free = [None]
        cs_tk = [None]

        def sa_load(c):
            sl = c % 2
            sa_ld[sl] = P.emit("sp", (lambda e, sl=sl, c=c: e.dma_start(
                out=SA[:, sl].rearrange("p s k -> p (s k)"), in_=sA[c * 128:(c + 1) * 128, :])),
                waits=[sa_free[sl]], sem="ld_sa%d" % sl, amount=16)

        def sample_conv(c, vs_tk):
            sl = c % 2
            d0 = P.emit("dve", (lambda e, sl=sl, c=c: e.tensor_tensor(
                out=SO[:, :, :], in0=SA[:, sl], in1=WDW[:, c, 0:30].unsqueeze(1).broadcast_to([128, NSAMP, 30]),
                op=ALU.mult)), waits=[sa_ld[sl], so_free[0]])
            P.wait("dve", d0)
            d1 = P.emit("dve", (lambda e: e.tensor_reduce(out=TA_s, in_=SO[:, :, :], axis=AX.X, op=ALU.add)))
            P.wait("dve", d1)
            d2 = P.emit("dve", (lambda e, c=c: e.scalar_tensor_tensor(
                out=TB_s, in0=VS[:, c, :], scalar=WDW[:, c, 30:31], in1=TA_s, op0=ALU.mult, op1=ALU.add)), waits=[vs_tk])
            P.wait("dve", d2)
            cs_tk[0] = P.emit("dve", (lambda e, c=c: e.tensor_scalar(
                out=CSB[:, c, :], in0=TB_s, scalar1=ccol(C_BDW + c), scalar2=None, op0=ALU.add)), waits=[sqs_free[0]])
            a1 = P.emit("act", (lambda e, sl=sl: e.activation(out=SO[:, :, 0:29], in_=SA[:, sl, :, 1:30], func=AF.Copy)),
                        waits=[d1])
            a2 = P.emit("act", (lambda e, c=c: e.activation(out=SO[:, :, 29], in_=VS[:, c, :], func=AF.Copy)), waits=[vs_tk])
            sa_free[sl] = a1
            o = P.emit("sp", (lambda e, c=c: e.dma_start(out=nas[c * 128:(c + 1) * 128, :],
                                                         in_=SO.rearrange("p s k -> p (s k)"))),
                       waits=[a2], sem="so", amount=16)
            so_free[0] = o
            out_tks.append(o)
            if c + 2 < KC:
                sa_load(c + 2)

        sa_load(0)
        sa_load(1)
        dgb_tk = None
        for c_ in range(KC):
            dgb_tk = P.emit("dve", (lambda e, c_=c_: e.tensor_tensor(
                out=DGB[:, c_], in0=ID32.unsqueeze(1).broadcast_to([128, CA_W, 32]),
                in1=WDW[:, c_, :].unsqueeze(2).broadcast_to([128, CA_W, 32]), op=ALU.mult)), waits=[prev_pe])
        vp_tk = [None] * KC
        for cp in range(4):
            ua, wav, lta = W.acquire("pw1a%d" % cp)
            ug, wgv, ltg = W.acquire("pw1g%d" % cp)
            for it_i, (cc, ti) in enumerate(FIRST_ORDER if cp == 0 else NORMAL_ORDER):
                if True:
                    c = cp * 2 + cc
                    t0, w = TILES[ti]
                    ba, bg = PS.get(2)
                    P.wait("pe", xn_tk[ti])
                    P.wait("pe", lta)
                    PS.wait_free(ba)
                    for k in range(KC):
                        P.emit("pe", (lambda e, k=k, cc=cc, t0=t0, w=w, ba=ba, wav=wav: e.matmul(
                            psum[:, ba, 0:w], wav[:, k, cc * 128:(cc + 1) * 128], XN[:, k, t0:t0 + w],
                            start=(k == 0), stop=(k == KC - 1))), sem=None)
                    P.wait("pe", ltg)
                    PS.wait_free(bg)
                    for k in range(KC):
                        pk = P.emit("pe", (lambda e, k=k, cc=cc, t0=t0, w=w, bg=bg, wgv=wgv: e.matmul(
                            psum[:, bg, 0:w], wgv[:, k, cc * 128:(cc + 1) * 128], XN[:, k, t0:t0 + w],
                            start=(k == 0), stop=(k == KC - 1))), sem="auto" if k == KC - 1 else None)
                    s = st_next()
                    a = P.emit("act", (lambda e, s=s, w=w, bg=bg, c=c: e.activation(
                        out=ST[:, s, 0:w], in_=psum[:, bg, 0:w], func=AF.Sigmoid, bias=ccol(C_BPW1 + 8 + c), scale=1.0)),
                        waits=[pk, st_free[s]])
                    PS.release(bg, a)
                    if ti < 4:
                        dd = P.emit("dve", (lambda e, s=s, ba=ba, c=c, t0=t0: e.scalar_tensor_tensor(
                            out=HB[:, c, 30 + t0:30 + t0 + 512], in0=psum[:, ba, :], scalar=ccol(C_BPW1 + c), in1=ST[:, s, :],
                            op0=ALU.add, op1=ALU.mult)), waits=[a, pk, zp])
                        if ti == 3:
                            dd = P.emit("dve", (lambda e, s=s, ba=ba, c=c: e.scalar_tensor_tensor(
                                out=VLAST[:, c, :], in0=psum[:, ba, 482:512], scalar=ccol(C_BPW1 + c), in1=ST[:, s, 482:512],
                                op0=ALU.add, op1=ALU.mult)))
                        v_tk = dd
                        vp_tk[c] = dd
                    else:
                        dd = P.emit("dve", (lambda e, s=s, ba=ba, c=c: e.scalar_tensor_tensor(
                            out=VS[:, c, :], in0=psum[:, ba, 0:NSAMP], scalar=ccol(C_BPW1 + c), in1=ST[:, s, 0:NSAMP],
                            op0=ALU.add, op1=ALU.mult)), waits=[a, pk])
                    PS.release(ba, dd)
                    st_free[s] = dd
                    last_pe = pk
                    if it_i in (1, 2) or (it_i == 0 and len(deferred) > 1):
                        flush_deferred()
                    if ti == 4:
                        sample_conv(c, dd)
            W.release(ua, last_pe)
            W.release(ug, last_pe, steal=(cp == 3))
        out_tks.append(P.emit("sp", lambda e: e.dma_start(out=nap.rearrange("(c p) k -> p c k", p=128), in_=VLAST),
                              waits=[v_tk], sem="out", amount=16))

        cb_free = {}
        q_free = [None, None]
        mv_free = [None]
        items = [(ti, c) for ti in range(4) for c in range(KC)]

        def src(ti, c, w):
            if ti == 4:
                return CSB[:, c, :]
            if c < 2 and ti % 2 == 1:
                return ST[:, c, 0:w]
            return CB[:, c, 0:w]

        def srckey(ti, c):
            return ("s", c) if ti == 4 else (("alt", c) if (c < 2 and ti % 2 == 1) else ("cb", c))

        qs_free = [None, None]

        def qbufs(ti):
            return (CQ_S, CBB_S, qs_free) if ti == 4 else (CQ, CBB, q_free)

        def stat_ops(ti, c, cev):
            w = TILES[ti][1]
            q = c % 2
            cq, cbb, qf = qbufs(ti)
            aq = P.emit("act", (lambda e, q=q, w=w, ti=ti, c=c, cq=cq: e.activation(out=cq[:, q, 0:w], in_=src(ti, c, w), func=AF.Square)),
                        waits=[cev, qf[q]])
            dq = P.emit("dve", (lambda e, q=q, w=w, ti=ti, c=c, cbb=cbb: e.tensor_copy(out=cbb[:, q, 0:w], in_=src(ti, c, w))),
                        waits=[cev, qf[q]])
            return (aq, dq)

        def stat_mm(ti, c, tks, banks):
            w = TILES[ti][1]
            q = c % 2
            s1b, s2b = banks
            if c == 0:
                PS.wait_free(s1b)
                PS.wait_free(s2b)
            P.wait("pe", tks[0])
            P.wait("pe", tks[1])
            cq, cbb, qf = qbufs(ti)
            P.emit("pe", (lambda e, c=c, q=q, w=w, s1b=s1b, cbb=cbb: e.matmul(psum[:, s1b, 0:w], ONES, cbb[:, q, 0:w],
                                                                              start=(c == 0), stop=(c == KC - 1))), sem=None)
            t = P.emit("pe", (lambda e, c=c, q=q, w=w, s2b=s2b, cq=cq: e.matmul(psum[:, s2b, 0:w], ONES, cq[:, q, 0:w],
                                                                                start=(c == 0), stop=(c == KC - 1))))
            qf[q] = t
            return t

        def chain(ti, st_last, banks):
            w = TILES[ti][1]
            s1b, s2b = banks
            m1 = P.emit("dve", (lambda e, w=w, s1b=s1b: e.tensor_scalar(out=MV[:, 0, 0:w], in0=psum[:, s1b, 0:w], scalar1=1.0 / D,
                                                                          scalar2=None, op0=ALU.mult)), waits=[st_last, mv_free[0]])
            P.wait("dve", m1)
            m2 = P.emit("dve", (lambda e, w=w: e.tensor_tensor(out=MV[:, 1, 0:w], in0=MV[:, 0, 0:w], in1=MV[:, 0, 0:w], op=ALU.mult)))
            P.wait("dve", m2)
            m3 = P.emit("dve", (lambda e, w=w, s2b=s2b: e.scalar_tensor_tensor(out=MV[:, 1, 0:w], in0=psum[:, s2b, 0:w], scalar=1.0 / D,
                                                                                 in1=MV[:, 1, 0:w], op0=ALU.mult, op1=ALU.subtract)))
            PS.release(s1b, m1)
            PS.release(s2b, m3)
            a3 = P.emit("act", (lambda e, w=w: e.activation(out=RS[:, 0, 0:w], in_=MV[:, 1, 0:w], func=AF.Ln, scale=1.0,
                                                            bias=ccol(C_EPS_LN))), waits=[m3, rs_free[0]])
            P.wait("act", a3)
            d3 = P.emit("act", (lambda e, w=w: e.activation(out=RS[:, 1, 0:w], in_=RS[:, 0, 0:w], func=AF.Exp, scale=-0.5)))
            return d3

        def apply(ti, c, d3):
            t0, w = TILES[ti]
            z1 = P.emit("dve", (lambda e, c=c, w=w, ti=ti: e.tensor_tensor(out=src(ti, c, w), in0=src(ti, c, w), in1=MV[:, 0, 0:w],
                                                                            op=ALU.subtract)), waits=[d3])
            P.wait("dve", z1)
            zz = P.emit("dve", (lambda e, c=c, w=w, ti=ti: e.tensor_tensor(out=src(ti, c, w), in0=src(ti, c, w), in1=RS[:, 1, 0:w],
                                                                            op=ALU.mult)))
            az = P.emit("act", (lambda e, c=c, t0=t0, w=w, ti=ti: e.activation(
                out=XN[:, c, t0:t0 + w], in_=src(ti, c, w), func=AF.Silu, bias=ccol(C_LNB + c), scale=ccol(C_LNG + c))),
                waits=[zz])
            cb_free[srckey(ti, c)] = az
            mv_free[0] = zz
            rs_free[0] = zz
            xn_tk[ti] = az
            return az

        def sample_tile():
            banks = tuple(PS.get(2))
            st_last = None
            for c in range(KC):
                tks = stat_ops(4, c, cs_tk[0])
                st_last = stat_mm(4, c, tks, banks)
            d3 = chain(4, st_last, banks)
            for c in range(KC):
                apply(4, c, d3)

        NPE = 28
        sslot = W.stolen[0]
        ACC = WR[:, sslot, 0:2048].bitcast(F32).rearrange("p (a b) -> p a b", a=2)
        acc_free = [W.stolen[1], W.stolen[1]]
        acc_tk = [None, None]

        def dve_taps(i_):
            ti_, c_ = items[i_]
            t0_ = TILES[ti_][0]
            j_ = i_ % 2
            tk_ = P.emit("dve", (lambda e, c_=c_, t0_=t0_, j_=j_: e.tensor_scalar(
                out=ACC[:, j_, :], in0=HB[:, c_, t0_ + NPE:t0_ + NPE + 512], scalar1=WDW[:, c_, NPE:NPE + 1], scalar2=None,
                op0=ALU.mult)), waits=[acc_free[j_], vp_tk[c_]])
            for k_ in range(NPE + 1, CA_W):
                P.wait("dve", tk_)
                tk_ = P.emit("dve", (lambda e, c_=c_, t0_=t0_, j_=j_, k_=k_: e.scalar_tensor_tensor(
                    out=ACC[:, j_, :], in0=HB[:, c_, t0_ + k_:t0_ + k_ + 512], scalar=WDW[:, c_, k_:k_ + 1], in1=ACC[:, j_, :],
                    op0=ALU.mult, op1=ALU.add)))
            acc_tk[j_] = tk_

        dve_taps(0)
        pending = None
        apply_q = []
        d3_of = {}
        for sl_ in range(2):
            P.wait("act", st_free[sl_])
            P.wait("dve", st_free[sl_])
        chain_pending = None
        for i, (ti, c) in enumerate(items):
            t0 = TILES[ti][0]
            (b,) = PS.get(1)
            PS.wait_free(b)
            P.wait("pe", dgb_tk)
            P.wait("pe", vp_tk[c])
            pk = None
            for k in range(NPE):
                for g in range(4):
                    last = (k == NPE - 1 and g == 3)
                    pk = P.emit("pe", (lambda e, k=k, g=g, c=c, t0=t0, b=b: e.matmul(
                        psum[32 * g:32 * g + 32, b, :], DGB[32 * g:32 * g + 32, c, k, :],
                        HB[32 * g:32 * g + 32, c, t0 + k:t0 + k + 512],
                        start=(k == 0), stop=(k == NPE - 1), tile_position=(32 * g, 32 * g))),
                        sem="auto" if last else None)
            last_pe = pk
            if i + 1 < len(items):
                dve_taps(i + 1)
            if i == 1:
                sample_tile()
            if c == 1 and chain_pending is not None:
                cti, cst = chain_pending
                chain_pending = None
                d3_of[cti] = chain(cti, cst, (6, 7))
                apply_q.extend((cti, cc) for cc in range(KC))
            if apply_q and c >= 2:
                for _ in range(3 if c == 2 else 1):
                    ati, ac = apply_q.pop(0)
                    apply(ati, ac, d3_of[ati])
            j = i % 2
            a = P.emit("dve", (lambda e, c=c, b=b, ti=ti, j=j: e.scalar_tensor_tensor(
                out=src(ti, c, 512), in0=psum[:, b, :], scalar=ccol(C_BDW + c), in1=ACC[:, j, :],
                op0=ALU.add, op1=ALU.add)),
                waits=[pk, cb_free.get(srckey(ti, c)), acc_tk[j]])
            PS.release(b, a)
            acc_free[j] = a
            tks = stat_ops(ti, c, a)
            if pending is not None:
                pti, pc, ptks = pending
                st_last = stat_mm(pti, pc, ptks, (6, 7))
                if pc == KC - 1:
                    chain_pending = (pti, st_last)
            pending = (ti, c, tks)
        pti, pc, ptks = pending
        st_last_f = stat_mm(pti, pc, ptks, (6, 7))
        W.give_back(acc_free[(len(items) - 1) % 2])
        W.enable_extra(last_pe)

        def m2_chain():
            P.wait("dve", sq_free[0])
            d3_of[pti] = chain(pti, st_last_f, (6, 7))

        tail_state = {"c": 0}

        def m2_tail_step():
            c_ = tail_state["c"]
            if c_ >= KC:
                return
            apply(pti, c_, d3_of[pti])
            tail_state["c"] = c_ + 1
            if c_ == KC - 1:
                st_free[0] = cb_free.get(("alt", 0))
                st_free[1] = cb_free.get(("alt", 1))
                sq_guard[0] = mv_free[0]

        def m2_tail():
            while tail_state["c"] < KC:
                m2_tail_step()

        us = [W.acquire("pw2_%d" % mp) for mp in range(4)]

        def tile_fn(ti):
            nonlocal last_pe
            t0, w = TILES[ti]
            for m in range(KC):
                _, w2v, lt = us[m // 2]
                mm = m % 2
                (b,) = PS.get(1)
                PS.wait_free(b)
                P.wait("pe", xn_tk[ti])
                P.wait("pe", lt)
                pk = None
                for k in range(KC):
                    pk = P.emit("pe", (lambda e, k=k, mm=mm, t0=t0, w=w, b=b, w2v=w2v: e.matmul(
                        psum[:, b, 0:w], w2v[:, k, mm * 128:(mm + 1) * 128], XN[:, k, t0:t0 + w],
                        start=(k == 0), stop=(k == KC - 1))), sem="auto" if k == KC - 1 else None)
                dd = P.emit("dve", (lambda e, m=m, t0=t0, w=w, b=b: e.scalar_tensor_tensor(
                    out=X[:, m, t0:t0 + w], in0=psum[:, b, 0:w], scalar=ccol(C_BPW2 + m), in1=X[:, m, t0:t0 + w],
                    op0=ALU.add, op1=ALU.add)), waits=[x_w.get((m, ti), x_ld[ti]), pk])
                PS.release(b, dd)
                x_w[(m, ti)] = dd
                x_tk[ti] = dd
                last_pe = pk
                if ti == 0:
                    m2_tail_step()
        if nxt is not None:
            nxt.alt_sq = True
        tiled_final_stage(tile_fn, nxt, hooks={0: m2_chain, 1: m2_tail})
        if nxt is not None:
            nxt.alt_sq = False
        for (u2, _, _) in us:
            W.release(u2, last_pe)
        return last_pe

    TA_s = view(O_TS, [NSAMP], F32)
    TB_s = view(O_TS + 64, [NSAMP], F32)

    def mixer_b(prev_pe, nxt):
        WCB = CONST[:, C_WCB:C_WCB + KC * CB_W].rearrange("p (c k) -> p c k", c=KC)
        sb_tk = P.emit("sp", lambda e: e.dma_start(out=SBI, in_=sB.rearrange("(c p) s -> p c s", p=128)),
                       waits=[prev_pe], sem="ld_sb", amount=16)
        SBI4 = SBI.rearrange("p c (s k) -> p c s k", k=2)
        SBO4 = SBO.rearrange("p c (s k) -> p c s k", k=2)
        last_pe = None
        y_tk = [None] * 5
        ub_free = [None, None]
        t_free = [None]
        tb_free = [None]
        tc_free = [None]
        hc_prev = [None]
        td_free = [None, None]
        td_rr = [0]
        for cp in range(4):
            uc_, wcv, ltc = W.acquire("binc%d" % cp)
            uh_, whv, lth = W.acquire("binh%d" % cp)
            ub_, wbv, ltb = W.acquire("binb%d" % cp)
            for cc in range(2):
                c = cp * 2 + cc
                for ti, (t0, w) in enumerate(TILES):
                    bb, bc, bh = PS.get(3)
                    P.wait("pe", xn_tk[ti])
                    pk = None
                    for (bk, wv, lt) in ((bc, wcv, ltc), (bh, whv, lth), (bb, wbv, ltb)):
                        P.wait("pe", lt)
                        PS.wait_free(bk)
                        for k in range(KC):
                            pk = P.emit("pe", (lambda e, k=k, cc=cc, t0=t0, w=w, bk=bk, wv=wv: e.matmul(
                                psum[:, bk, 0:w], wv[:, k, cc * 128:(cc + 1) * 128], XN[:, k, t0:t0 + w],
                                start=(k == 0), stop=(k == KC - 1))),
                                sem="auto" if (k == KC - 1) else None)
                        if bk == bc:
                            pkc = pk
                        elif bk == bh:
                            pkh = pk
                    s = st_next()
                    a = P.emit("act", (lambda e, s=s, w=w, bc=bc: e.activation(out=ST[:, s, 0:w], in_=psum[:, bc, 0:w], func=AF.Copy)),
                               waits=[pkc, st_free[s]])
                    PS.release(bc, a)
                    di = td_rr[0]
                    td_rr[0] ^= 1
                    ab = P.emit("act", (lambda e, di=di, w=w, bb=bb: e.activation(out=TD[:, di, 0:w], in_=psum[:, bb, 0:w], func=AF.Copy)),
                                waits=[pk, td_free[di]])
                    PS.release(bb, ab)
                    if ti < 4:
                        ui = ti % 2
                        hz = None
                        if ti == 0:
                            hz = P.emit("dve", (lambda e: e.memset(UB[:, 0, 0:2], 0.0)), waits=[ub_free[0]])
                        du = P.emit("dve", (lambda e, s=s, ui=ui, bh=bh: e.tensor_tensor(
                            out=UB[:, ui, 2:514], in0=ST[:, s, :], in1=psum[:, bh, :], op=ALU.mult)),
                            waits=[a, pkh, ub_free[ui]])
                        PS.release(bh, du)
                        st_free[s] = du
                        if ti < 3:
                            hc = P.emit("act", (lambda e, ui=ui: e.activation(out=UB[:, 1 - ui, 0:2], in_=UB[:, ui, 512:514], func=AF.Copy)),
                                        waits=[du, ub_free[1 - ui]])
                        else:
                            hc = P.emit("act", (lambda e, ui=ui, c=c: e.activation(out=UBL[:, c, :], in_=UB[:, ui, 512:514], func=AF.Copy)),
                                        waits=[du])
                        s1 = P.emit("act", (lambda e, ui=ui, c=c: e.activation(
                            out=TA, in_=UB[:, ui, 2:514], func=AF.Copy, scale=WCB[:, c, 2:3])),
                            waits=[du, t_free[0]])
                        t2 = P.emit("dve", (lambda e, ui=ui, c=c: e.scalar_tensor_tensor(
                            out=TB, in0=UB[:, ui, 1:513], scalar=WCB[:, c, 1:2], in1=TA, op0=ALU.mult, op1=ALU.add)),
                            waits=[s1, hc_prev[0], hz])
                        P.wait("dve", t2)
                        t3 = P.emit("dve", (lambda e, ui=ui, c=c: e.scalar_tensor_tensor(
                            out=TA, in0=UB[:, ui, 0:512], scalar=WCB[:, c, 0:1], in1=TB, op0=ALU.mult, op1=ALU.add)))
                        P.wait("dve", t3)
                        hc_prev[0] = hc
                        ub_free[ui] = t3
                        yy = P.emit("dve", (lambda e, c=c, t0=t0, di=di: e.tensor_tensor(
                            out=HB[:, c, t0:t0 + 512], in0=TA, in1=TD[:, di, :], op=ALU.mult)), waits=[ab])
                        t_free[0] = yy
                        td_free[di] = yy
                    else:
                        du = P.emit("dve", (lambda e, s=s, c=c, bh=bh: e.tensor_tensor(
                            out=US[:, c, :], in0=ST[:, s, 0:NSAMP], in1=psum[:, bh, 0:NSAMP], op=ALU.mult)),
                            waits=[a, pkh, sb_tk])
                        PS.release(bh, du)
                        st_free[s] = du
                        s1 = P.emit("act", (lambda e, c=c: e.activation(
                            out=TA[:, 0:NSAMP], in_=US[:, c, :], func=AF.Copy, scale=WCB[:, c, 2:3])),
                            waits=[du, t_free[0], sb_tk])
                        t2 = P.emit("dve", (lambda e, c=c: e.scalar_tensor_tensor(
                            out=TB[:, 0:NSAMP], in0=SBI4[:, c, :, 1], scalar=WCB[:, c, 1:2], in1=TA[:, 0:NSAMP],
                            op0=ALU.mult, op1=ALU.add)), waits=[s1])
                        P.wait("dve", t2)
                        t3 = P.emit("dve", (lambda e, c=c: e.scalar_tensor_tensor(
                            out=TA[:, 0:NSAMP], in0=SBI4[:, c, :, 0], scalar=WCB[:, c, 0:1], in1=TB[:, 0:NSAMP],
                            op0=ALU.mult, op1=ALU.add)))
                        P.wait("dve", t3)
                        yy = P.emit("dve", (lambda e, c=c, t0=t0, di=di: e.tensor_tensor(
                            out=HB[:, c, t0:t0 + NSAMP], in0=TA[:, 0:NSAMP], in1=TD[:, di, 0:NSAMP], op=ALU.mult)), waits=[ab])
                        t_free[0] = yy
                        td_free[di] = yy
                        P.emit("act", (lambda e, c=c: e.activation(out=SBO4[:, c, :, 0], in_=SBI4[:, c, :, 1], func=AF.Copy)),
                               waits=[sb_tk])
                        sbo_tk = P.emit("act", (lambda e, c=c: e.activation(out=SBO4[:, c, :, 1], in_=US[:, c, :], func=AF.Copy)),
                                        waits=[du])
                        y_tk[ti] = sbo_tk
                    xn_tk_b[ti] = yy
                    last_pe = pk
                    if ti in (0, 1):
                        flush_deferred()
            W.release(uc_, last_pe)
            W.release(uh_, last_pe)
            W.release(ub_, last_pe)
        out_tks.append(P.emit("sp", lambda e: e.dma_start(out=nbp.rearrange("(c p) k -> p c k", p=128), in_=UBL),
                              waits=[xn_tk_b[3], ("p_act", P.cnt["p_act"])], sem="out", amount=16))
        out_tks.append(P.emit("sp", lambda e: e.dma_start(out=nbs.rearrange("(c p) s -> p c s", p=128), in_=SBO),
                              waits=[y_tk[4]], sem="out", amount=16))
        us = [W.acquire("bout%d" % mp) for mp in range(4)]

        def tile_fn(ti):
            nonlocal last_pe
            t0, w = TILES[ti]
            for m in range(KC):
                _, wov, lt = us[m // 2]
                mm = m % 2
                (b,) = PS.get(1)
                PS.wait_free(b)
                P.wait("pe", xn_tk_b[ti])
                P.wait("pe", lt)
                pk = None
                for k in range(KC):
                    pk = P.emit("pe", (lambda e, k=k, mm=mm, t0=t0, w=w, b=b, wov=wov: e.matmul(
                        psum[:, b, 0:w], wov[:, k, mm * 128:(mm + 1) * 128], HB[:, k, t0:t0 + w],
                        start=(k == 0), stop=(k == KC - 1))), sem="auto" if k == KC - 1 else None)
                dd = P.emit("dve", (lambda e, m=m, t0=t0, w=w, b=b: e.tensor_tensor(
                    out=X[:, m, t0:t0 + w], in0=psum[:, b, 0:w], in1=X[:, m, t0:t0 + w], op=ALU.add)),
                    waits=[x_w.get((m, ti), x_ld[ti]), pk])
                PS.release(b, dd)
                x_w[(m, ti)] = dd
                x_tk[ti] = dd
                last_pe = pk
        tiled_final_stage(tile_fn, nxt, order=[0, 4, 1, 2, 3])
        for (uo, _, _) in us:
            W.release(uo, last_pe)
        return last_pe

    xn_tk_b = [None] * 5
    ydst = yT.rearrange("(c p) t -> p c t", p=128)

    norms = [Norm(C_FFN1N[0]), Norm(C_MIXN[0]), Norm(C_FFN2N[0]), Norm(C_FFN1N[1]), Norm(C_MIXN[1]), Norm(C_FFN2N[1]),
             Norm(C_FINN, final=True)]

    def nx(i):
        return norms[i + 1] if i + 1 <= stop_after else None

    last = None
    for ti_ in (0, 1):
        norms[0].part1(ti_)
        norms[0].part2(ti_)
    norms[0].override = {3: XN[:, :, 1536:2048]}
    for ti_ in (2, 3, 4):
        norms[0].part1a(ti_)
        deferred.append((norms[0], ti_, False))
    if stop_after >= 0:
        last = ffn("f10", nx(0))
    if stop_after >= 1:
        last = mixer_a(last, nx(1))
    if stop_after >= 2:
        last = ffn("f20", nx(2))
    if stop_after >= 3:
        last = ffn("f11", nx(3))
    if stop_after >= 4:
        last = mixer_b(last, nx(4))
    if stop_after >= 5:
        last = ffn("f21", nx(5))
    while deferred:
        flush_deferred()
    if stop_after < 6:
        for ti, (t0, w) in enumerate(TILES):
            out_tks.append(P.emit("sp", (lambda e, t0=t0, w=w: e.dma_start(out=ydst[:, :, t0:t0 + w], in_=X[:, :, t0:t0 + w])),
                                  waits=[x_tk[ti]], sem="out", amount=16))
    for sem in ("out", "so"):
        if sem in P.cnt:
            P.wait("sp", (sem, P.cnt[sem]))
    assert W.cursor == len(W.units) and W.next_dma == len(W.units)


    sem_names = sorted(P.cnt.keys())
    sems = {n: es.enter_context(nc.semaphore(n)) for n in sem_names}
    handles = {"pe": None, "act": None, "dve": None, "pool": None, "sp": None}

    def replay(name, eng):
        fuse = name in ("pe", "act", "dve")
        ops = P.ops[name]
        pend = []
        for op in ops:
            if op[0] == "wait":
                pend.append(op)
                continue
            fused = None
            if fuse and pend:
                fused = pend.pop()
            for w_ in pend:
                eng.wait_ge(sems[w_[1]], w_[2])
            pend = []
            ins = op[1](eng)
            if fused is not None:
                ins._wait_ge(sems[fused[1]], fused[2])
            if op[2] is not None:
                ins.then_inc(sems[op[2][0]], op[2][1])
        for w_ in pend:
            eng.wait_ge(sems[w_[1]], w_[2])

    with nc.Block() as block:
        @block.tensor
        def _(e):
            replay("pe", e)

        @block.scalar
        def _(e):
            replay("act", e)

        @block.vector
        def _(e):
            replay("dve", e)

        @block.gpsimd
        def _(e):
            replay("pool", e)

        @block.sync
        def _(e):
            replay("sp", e)
    es.close()
    return nc


def _pack_consts(inp):
    f = np.float32

    def fm(v):
        return np.asarray(v, f).reshape(KC, 128).T

    c = np.zeros((128, NCONST), f)
    for l in range(2):
        c[:, C_FFN1N[l]:C_FFN1N[l] + 8] = fm(inp["ffn1_norm"][l])
        c[:, C_MIXN[l]:C_MIXN[l] + 8] = fm(inp["mix_norm"][l])
        c[:, C_FFN2N[l]:C_FFN2N[l] + 8] = fm(inp["ffn2_norm"][l])
    c[:, C_FINN:C_FINN + 8] = fm(inp["final_norm"])
    c[:, C_BPW1:C_BPW1 + 16] = np.asarray(inp["a_b_pw1"][0], f).reshape(16, 128).T
    c[:, C_BDW:C_BDW + 8] = fm(inp["a_b_dw"][0])
    c[:, C_LNG:C_LNG + 8] = fm(inp["a_ln_g"][0])
    c[:, C_LNB:C_LNB + 8] = fm(inp["a_ln_b"][0])
    c[:, C_BPW2:C_BPW2 + 8] = fm(inp["a_b_pw2"][0])
    wdw = np.asarray(inp["a_w_dw"][0], f)
    c[:, C_WDW:C_WDW + KC * CA_W] = wdw.reshape(CA_W, KC, 128).transpose(2, 1, 0).reshape(128, KC * CA_W)
    wcb = np.asarray(inp["b_w_conv"][0], f)
    c[:, C_WCB:C_WCB + KC * CB_W] = wcb.reshape(CB_W, KC, 128).transpose(2, 1, 0).reshape(128, KC * CB_W)
    c[:, C_ID32:C_ID32 + 32] = (np.arange(32)[None, :] == (np.arange(128) % 32)[:, None]).astype(f)
    c[:, C_EPS_RMS] = RMS_EPS
    c[:, C_EPS_LN] = LN_EPS
    return c


_NC_CACHE = {}


def _run(inp, stop_after=6, trace=False):
    if stop_after not in _NC_CACHE:
        _NC_CACHE[stop_after] = build_program(stop_after)
    nc = _NC_CACHE[stop_after]
    f = np.float32
    consts = _pack_consts(inp)
    xp = np.asarray(inp["x_prompt"], f)
    xs = np.asarray(inp["x_sample"], f)[:, 0, :]
    sa = np.asarray(inp["state_conv_a"], f)[0]
    sb = np.asarray(inp["state_conv_b"], f)[0]
    shared = {k: np.ascontiguousarray(np.asarray(inp[k], f)) for k in
              ("ffn1_w_gate", "ffn1_w_up", "ffn1_w_down", "ffn2_w_gate", "ffn2_w_up", "ffn2_w_down")}
    shared["a_w_pw1"] = np.ascontiguousarray(np.asarray(inp["a_w_pw1"], f)[0])
    shared["a_w_pw2"] = np.ascontiguousarray(np.asarray(inp["a_w_pw2"], f)[0])
    shared["b_w_in"] = np.ascontiguousarray(np.asarray(inp["b_w_in"], f)[0])
    shared["b_w_out"] = np.ascontiguousarray(np.asarray(inp["b_w_out"], f)[0])
    in_maps = []
    for b in range(NCORES):
        sl = slice(b * NSAMP, (b + 1) * NSAMP)
        xt = np.empty((D, TT), f)
        xt[:, :SEQ] = xp[b].T
        xt[:, SEQ:] = xs[sl].T
        m = dict(shared)
        m["xT"] = xt
        m["consts"] = consts
        m["sA"] = np.ascontiguousarray(sa[sl].transpose(2, 0, 1)).reshape(D, NSAMP * 30)
        m["sB"] = np.ascontiguousarray(sb[sl].transpose(2, 0, 1)).reshape(D, NSAMP * 2)
        in_maps.append(m)
    res = run_bass_kernel_spmd(nc, in_maps, core_ids=list(range(NCORES)), trace=trace)
    return res


def kernel(**inputs):
    res = _run(inputs)
    f = np.float32
    y_prompt = np.empty((NCORES, SEQ, D), f)
    y_sample = np.empty((NCORES * NSAMP, 1, D), f)
    na_p = np.empty((1, NCORES, 30, D), f)
    na_s = np.empty((1, NCORES * NSAMP, 30, D), f)
    nb_p = np.empty((1, NCORES, 2, D), f)
    nb_s = np.empty((1, NCORES * NSAMP, 2, D), f)
    for b, r in enumerate(res.results):
        sl = slice(b * NSAMP, (b + 1) * NSAMP)
        yt = np.asarray(r["yT"], f)
        y_prompt[b] = yt[:, :SEQ].T
        y_sample[sl, 0, :] = yt[:, SEQ:].T
        na_p[0, b] = np.asarray(r["nap"], f).T
        na_s[0, sl] = np.asarray(r["nas"], f).reshape(D, NSAMP, 30).transpose(1, 2, 0)
        nb_p[0, b] = np.asarray(r["nbp"], f).T
        nb_s[0, sl] = np.asarray(r["nbs"], f).reshape(D, NSAMP, 2).transpose(1, 2, 0)
    return (y_prompt, y_sample, na_p, na_s, nb_p, nb_s)
```
